# Optimizing a Trainium2 kernel written in Bass

```python
import jax, jax.numpy as jnp
from jax import lax
import numpy as np

D_MODEL = 1024
BATCH = 16
SEQ = 4096
DEPTH = 2
DEC_BATCH = 8
DEC_SEQ = 2048
PAST_LEN = 128

HEAD_DIM = 64
RET_HEADS = 4
FFT_GROUPS = 4
ATT_Q_HEADS = 4
ATT_KV_HEADS = 2
CONV_GROUPS = 4
RET_W = RET_HEADS * HEAD_DIM
FFT_W = FFT_GROUPS * HEAD_DIM
ATT_W = ATT_Q_HEADS * HEAD_DIM
ATT_KV_W = ATT_KV_HEADS * HEAD_DIM
CONV_W = CONV_GROUPS * HEAD_DIM
MIX_W = RET_W + FFT_W + ATT_W + CONV_W
PROJ_SPLITS = (RET_W, RET_W, RET_W, RET_W, FFT_W, ATT_W, ATT_KV_W, ATT_KV_W, CONV_W, CONV_W, CONV_W)
PROJ_W = 4 * RET_W + FFT_W + ATT_W + 2 * ATT_KV_W + 3 * CONV_W
D_FF = 4 * D_MODEL
PLE_DIM = 256
GRID_W = 64
RET_CHUNK = 128
Q_BLOCK = 128
CONV_WIDTH = 3
ROPE_BASE = 10000.0
NORM_EPS = 1e-6

kernel_name = 'hybrid_parallel_heads_bidir_encoder'


def _rmsnorm(x, w):
    xf = x.astype(jnp.float32)
    y = xf * lax.rsqrt(jnp.mean(xf * xf, axis=-1, keepdims=True) + NORM_EPS)
    return (y * w.astype(jnp.float32)).astype(x.dtype)


def _rope(x, ang):
    m = ang.shape[-1]
    cos = jnp.cos(ang)[None, :, None, :]
    sin = jnp.sin(ang)[None, :, None, :]
    x1, x2 = x[..., :m], x[..., m:]
    return jnp.concatenate([x1 * cos - x2 * sin, x2 * cos + x1 * sin], axis=-1)


def _retention_dir(q, k, v, log_g, strict):
    b, h, s, d = q.shape
    nc = s // RET_CHUNK
    qc = q.reshape(b, h, nc, RET_CHUNK, d)
    kc = k.reshape(b, h, nc, RET_CHUNK, d)
    vc = v.reshape(b, h, nc, RET_CHUNK, d)
    idx = jnp.arange(RET_CHUNK, dtype=jnp.float32)
    diff = idx[:, None] - idx[None, :]
    mask = (diff > 0) if strict else (diff >= 0)
    decay = jnp.where(mask[None], jnp.exp(jnp.where(mask, diff, 0.0)[None] * log_g[:, None, None]), 0.0)
    scores = jnp.einsum('bhncd,bhnmd->bhncm', qc, kc) * decay[None, :, None]
    inner = jnp.einsum('bhncm,bhnmd->bhncd', scores, vc)
    zeta = jnp.exp((RET_CHUNK - 1 - idx)[None, :] * log_g[:, None])
    kv = jnp.einsum('bhncd,bhnce->nbhde', kc * zeta[None, :, None, :, None], vc)
    g_chunk = jnp.exp(RET_CHUNK * log_g)[None, :, None, None]

    def step(state, kv_n):
        return state * g_chunk + kv_n, state

    _, prev = lax.scan(step, jnp.zeros((b, h, d, d), jnp.float32), kv)
    xi = jnp.exp((idx + 1.0)[None, :] * log_g[:, None])
    cross = jnp.einsum('bhncd,nbhde->bhnce', qc, prev) * xi[None, :, None, :, None]
    return (inner + cross).reshape(b, h, s, d)


def _retention_mixer(q, k, v, g, log_rate, gn_w):
    b, s, _ = q.shape
    dt = q.dtype
    t = jnp.arange(s, dtype=jnp.float32)
    inv = ROPE_BASE ** (-jnp.arange(HEAD_DIM // 2, dtype=jnp.float32) / (HEAD_DIM // 2))
    ang = t[:, None] * inv[None, :]
    qh = _rope(q.astype(jnp.float32).reshape(b, s, RET_HEADS, HEAD_DIM), ang)
    kh = _rope(k.astype(jnp.float32).reshape(b, s, RET_HEADS, HEAD_DIM), ang) * (HEAD_DIM ** -0.5)
    vh = v.astype(jnp.float32).reshape(b, s, RET_HEADS, HEAD_DIM)
    qh, kh, vh = (a.transpose(0, 2, 1, 3) for a in (qh, kh, vh))
    log_g = -jnp.exp(log_rate.astype(jnp.float32))
    fwd = _retention_dir(qh, kh, vh, log_g[0], False)
    bwd = jnp.flip(_retention_dir(jnp.flip(qh, 2), jnp.flip(kh, 2), jnp.flip(vh, 2), log_g[1], True), 2)
    o = (fwd + bwd).transpose(0, 2, 1, 3)
    mu = jnp.mean(o, axis=-1, keepdims=True)
    var = jnp.mean(jnp.square(o - mu), axis=-1, keepdims=True)
    o = ((o - mu) * lax.rsqrt(var + NORM_EPS)).reshape(b, s, RET_W) * gn_w.astype(jnp.float32)
    return (jax.nn.silu(g.astype(jnp.float32)) * o).astype(dt)


def _fourier_mixer(u):
    b, s, _ = u.shape
    uf = u.astype(jnp.float32).reshape(b, s, FFT_GROUPS, HEAD_DIM)
    y = jnp.real(jnp.fft.fft2(uf, axes=(1, 3), norm='ortho'))
    return y.reshape(b, s, FFT_W).astype(u.dtype)


def _attention_mixer(q, k, v, q_norm_w, k_norm_w):
    b, s, _ = q.shape
    dt = q.dtype
    grp = ATT_Q_HEADS // ATT_KV_HEADS
    rows = s // GRID_W
    r_idx, c_idx = jnp.meshgrid(jnp.arange(rows, dtype=jnp.float32), jnp.arange(GRID_W, dtype=jnp.float32), indexing='ij')
    r_idx = r_idx.reshape(-1)
    c_idx = c_idx.reshape(-1)
    half = HEAD_DIM // 2
    inv = ROPE_BASE ** (-jnp.arange(half // 2, dtype=jnp.float32) / (half // 2))
    ang_r = r_idx[:, None] * inv[None, :]
    ang_c = c_idx[:, None] * inv[None, :]

    def axial(a):
        af = a.astype(jnp.float32)
        return jnp.concatenate([_rope(af[..., :half], ang_r), _rope(af[..., half:], ang_c)], axis=-1).astype(dt)

    qh = axial(_rmsnorm(q.reshape(b, s, ATT_Q_HEADS, HEAD_DIM), q_norm_w))
    kh = axial(_rmsnorm(k.reshape(b, s, ATT_KV_HEADS, HEAD_DIM), k_norm_w))
    vh = v.reshape(b, s, ATT_KV_HEADS, HEAD_DIM)
    qh = qh.reshape(b, s, ATT_KV_HEADS, grp, HEAD_DIM).transpose(0, 2, 3, 1, 4)
    kh = kh.transpose(0, 2, 1, 3)
    vh = vh.transpose(0, 2, 1, 3)
    nb = s // Q_BLOCK
    qb = jnp.moveaxis(qh.reshape(b, ATT_KV_HEADS, grp, nb, Q_BLOCK, HEAD_DIM), 3, 0)
    scale = HEAD_DIM ** -0.5

    def attend(qblk):
        sc = jnp.einsum('bkgqd,bksd->bkgqs', qblk, kh).astype(jnp.float32) * scale
        pr = jax.nn.softmax(sc, axis=-1)
        return jnp.einsum('bkgqs,bksd->bkgqd', pr.astype(vh.dtype), vh)

    o = lax.map(attend, qb)
    o = jnp.moveaxis(o, 0, 3).reshape(b, ATT_KV_HEADS, grp, s, HEAD_DIM)
    return o.transpose(0, 3, 1, 2, 4).reshape(b, s, ATT_W)


def _conv_mixer(b_gate, c_gate, h, conv_w):
    s = h.shape[1]
    pad = CONV_WIDTH // 2
    z = c_gate * h
    zp = jnp.pad(z, ((0, 0), (pad, pad), (0, 0)))
    y = zp[:, 0:s] * conv_w[0]
    for j in range(1, CONV_WIDTH):
        y = y + zp[:, j:j + s] * conv_w[j]
    return b_gate * y


def _encode(x, p, norm_mix_w, w_in, ret_log_rate, ret_gn_w, q_norm_w, k_norm_w, conv_w, w_out,
            norm_ffn_w, w_ffn1, w_ffn2, norm_pl_w, w_pl_gate, w_pl_proj, final_norm_w):
    offsets = np.cumsum(PROJ_SPLITS)[:-1].tolist()
    for l in range(DEPTH):
        h = _rmsnorm(x, norm_mix_w[l])
        proj = h @ w_in[l]
        rq, rk, rv, rg, fu, aq, ak, av, cb, cc, ch = jnp.split(proj, offsets, axis=-1)
        y_ret = _retention_mixer(rq, rk, rv, rg, ret_log_rate[l], ret_gn_w[l])
        y_fft = _fourier_mixer(fu)
        y_att = _attention_mixer(aq, ak, av, q_norm_w[l], k_norm_w[l])
        y_conv = _conv_mixer(cb, cc, ch, conv_w[l])
        mixed = jnp.concatenate([y_ret, y_fft, y_att, y_conv], axis=-1)
        x = x + mixed @ w_out[l]
        hf = _rmsnorm(x, norm_ffn_w[l]) @ w_ffn1[l]
        x = x + jnp.square(jax.nn.relu(hf)) @ w_ffn2[l]
        gate = jax.nn.sigmoid(_rmsnorm(x, norm_pl_w[l]) @ w_pl_gate[l])
        x = x + gate * (p[l] @ w_pl_proj[l])
    return _rmsnorm(x, final_norm_w)


def setup_inputs(seed: int = 0) -> dict:
    key = jax.random.key(seed)
    ks = jax.random.split(key, 20)
    f32 = jnp.float32
    nrm = lambda k, shape, scale: jax.random.normal(k, shape, f32) * scale
    base_rate = -(5.0 + jnp.arange(RET_HEADS, dtype=f32)) * jnp.log(2.0)
    return {
        'x_prompt': nrm(ks[0], (BATCH, SEQ, D_MODEL), 1.0),
        'x_sample': nrm(ks[1], (DEC_BATCH, DEC_SEQ, D_MODEL), 1.0),
        'p_prompt': nrm(ks[2], (DEPTH, BATCH, SEQ, PLE_DIM), 1.0),
        'p_sample': nrm(ks[3], (DEPTH, DEC_BATCH, DEC_SEQ, PLE_DIM), 1.0),
        'norm_mix_w': 1.0 + nrm(ks[4], (DEPTH, D_MODEL), 0.02),
        'w_in': nrm(ks[5], (DEPTH, D_MODEL, PROJ_W), D_MODEL ** -0.5),
        'ret_log_rate': base_rate[None, None, :] + nrm(ks[6], (DEPTH, 2, RET_HEADS), 0.1),
        'ret_gn_w': 1.0 + nrm(ks[7], (DEPTH, RET_W), 0.02),
        'q_norm_w': 1.0 + nrm(ks[8], (DEPTH, HEAD_DIM), 0.02),
        'k_norm_w': 1.0 + nrm(ks[9], (DEPTH, HEAD_DIM), 0.02),
        'conv_w': nrm(ks[10], (DEPTH, CONV_WIDTH, CONV_W), CONV_WIDTH ** -0.5),
        'w_out': nrm(ks[11], (DEPTH, MIX_W, D_MODEL), MIX_W ** -0.5),
        'norm_ffn_w': 1.0 + nrm(ks[12], (DEPTH, D_MODEL), 0.02),
        'w_ffn1': nrm(ks[13], (DEPTH, D_MODEL, D_FF), D_MODEL ** -0.5),
        'w_ffn2': nrm(ks[14], (DEPTH, D_FF, D_MODEL), D_FF ** -0.5),
        'norm_pl_w': 1.0 + nrm(ks[15], (DEPTH, D_MODEL), 0.02),
        'w_pl_gate': nrm(ks[16], (DEPTH, D_MODEL, D_MODEL), D_MODEL ** -0.5),
        'w_pl_proj': nrm(ks[17], (DEPTH, PLE_DIM, D_MODEL), PLE_DIM ** -0.5),
        'final_norm_w': 1.0 + nrm(ks[18], (D_MODEL,), 0.02),
    }


def reference(x_prompt, x_sample, p_prompt, p_sample, norm_mix_w, w_in, ret_log_rate, ret_gn_w,
              q_norm_w, k_norm_w, conv_w, w_out, norm_ffn_w, w_ffn1, w_ffn2, norm_pl_w, w_pl_gate,
              w_pl_proj, final_norm_w):
    y_prompt = _encode(x_prompt, p_prompt, norm_mix_w, w_in, ret_log_rate, ret_gn_w, q_norm_w, k_norm_w,
                       conv_w, w_out, norm_ffn_w, w_ffn1, w_ffn2, norm_pl_w, w_pl_gate, w_pl_proj, final_norm_w)
    y_sample = _encode(x_sample, p_sample, norm_mix_w, w_in, ret_log_rate, ret_gn_w, q_norm_w, k_norm_w,
                       conv_w, w_out, norm_ffn_w, w_ffn1, w_ffn2, norm_pl_w, w_pl_gate, w_pl_proj, final_norm_w)
    return (y_prompt, y_sample)
```

```python
import contextlib
import numpy as np
import ml_dtypes
import concourse.bass as bass
import concourse.mybir as mybir
from concourse.bass_utils import run_bass_kernel_spmd

F32 = mybir.dt.float32
BF16 = mybir.dt.bfloat16
AF = mybir.ActivationFunctionType
ALU = mybir.AluOpType

D = 1024
DFF = 4096
PLE = 256
EPS = 1e-6
NCORES = 8
ENGS = ("pe", "act", "dve", "pool", "sp")
N_DMA_SEMS = 8


class Prog:
    def __init__(self, nc):
        self.nc = nc
        self.ops = {e: [] for e in ENGS}
        self.last_w = {}
        self.readers = {}
        self.ndma = {e: 0 for e in ENGS}
        self.last_dma = {}

    def emit(self, eng, fn, reads=(), writes=(), dma=False, extra_deps=(), grp=0):
        ops = self.ops[eng]
        idx = len(ops)
        deps = {}

        def add(d):
            e, i = d
            pd = self.ops[e][i]["dma"]
            if pd:
                key = (e, "d", self.ops[e][i]["dsem"])
            else:
                key = (e, "c")
            if deps.get(key, (None, -1))[1] < i:
                deps[key] = (e, i)

        for k in reads:
            lw = self.last_w.get(k)
            if lw is not None:
                if lw[0] == eng and eng == "pe" and not dma and not self.ops[lw[0]][lw[1]]["dma"]:
                    continue
                add(lw)
        for k in writes:
            lw = self.last_w.get(k)
            if lw is not None:
                if not (lw[0] == eng and eng == "pe" and not dma and not self.ops[lw[0]][lw[1]]["dma"]):
                    add(lw)
            for e, i in self.readers.get(k, {}).items():
                if e == eng and eng == "pe" and not dma and not self.ops[e][i]["dma"]:
                    continue
                add((e, i))
        for d in extra_deps:
            add(d)
        op = dict(fn=fn, deps=list(deps.values()), dma=dma, inc=False)
        if dma:
            cnt = self.ndma.get((eng, grp), 0)
            op["dsem"] = grp * N_DMA_SEMS + cnt % N_DMA_SEMS
            self.ndma[(eng, grp)] = cnt + 1
            self.last_dma[(eng, op["dsem"])] = idx
        ops.append(op)
        for k in writes:
            self.last_w[k] = (eng, idx)
            self.readers[k] = {}
        for k in reads:
            self.readers.setdefault(k, {})[eng] = idx
        return (eng, idx)

    @staticmethod
    def _mk(meth, a, kw):
        return lambda eo: getattr(eo, meth)(*a, **kw)

    def pe(self, reads, writes, meth, *a, **kw):
        return self.emit("pe", self._mk(meth, a, kw), reads, writes)

    def act(self, reads, writes, meth, *a, **kw):
        return self.emit("act", self._mk(meth, a, kw), reads, writes)

    def dve(self, reads, writes, meth, *a, **kw):
        return self.emit("dve", self._mk(meth, a, kw), reads, writes)

    def pool(self, reads, writes, meth, *a, **kw):
        return self.emit("pool", self._mk(meth, a, kw), reads, writes)

    def dma(self, q, reads, writes, out, in_, grp=0):
        return self.emit(q, self._mk("dma_start", (), dict(out=out, in_=in_)), reads, writes, dma=True, grp=grp)

    def barrier(self):
        marks = []
        for e in ENGS:
            if e == "pool" and getattr(self, "dummy", None) is not None:
                marks.append(self.emit(e, self._mk("memset", (self.dummy, 0.0), {})))
            else:
                marks.append(self.emit(e, lambda eo: eo.drain()))
        dmas = [(e, i) for (e, s_), i in self.last_dma.items() if s_ < N_DMA_SEMS]
        for e in ENGS:
            self.emit(e, None, extra_deps=[m for m in marks if m[0] != e] + dmas)
        self.last_w = {k: v for k, v in self.last_w.items() if k.startswith("W_")}
        self.readers = {}

    def finalize(self):
        nc = self.nc
        dmas = [(e, i) for (e, _s), i in self.last_dma.items()]
        self.emit("sp", None, extra_deps=dmas)
        for e in ENGS:
            for op in self.ops[e]:
                for pe_, pi in op["deps"]:
                    self.ops[pe_][pi]["inc"] = True
        for e in ENGS:
            c = 0
            dcount = [0] * (2 * N_DMA_SEMS)
            for op in self.ops[e]:
                if op["dma"]:
                    dcount[op["dsem"]] += 16
                    op["dval"] = dcount[op["dsem"]]
                elif op["inc"]:
                    c += 1
                    op["cval"] = c
        stack = contextlib.ExitStack()
        csem = {e: stack.enter_context(nc.semaphore("c_" + e)) for e in ENGS}
        dsem = {e: [stack.enter_context(nc.semaphore("d_%s%d" % (e, i))) for i in range(2 * N_DMA_SEMS)]
                for e in ("sp", "pool")}
        block = stack.enter_context(nc.Block())
        ops_all = self.ops
        nwaits = {e: 0 for e in ENGS}

        def run(e, eo):
            waited = {}
            for op in ops_all[e]:
                for pe_, pi in op["deps"]:
                    pop = ops_all[pe_][pi]
                    if pop["dma"]:
                        s = dsem[pe_][pop["dsem"]]
                        v = pop["dval"]
                        key = ("d", pe_, pop["dsem"])
                    else:
                        s = csem[pe_]
                        v = pop["cval"]
                        key = ("c", pe_)
                    if waited.get(key, 0) >= v:
                        continue
                    waited[key] = v
                    eo.wait_ge(s, v)
                    nwaits[e] += 1
                if op["fn"] is None:
                    continue
                ins = op["fn"](eo)
                if op["dma"]:
                    ins.then_inc(dsem[e][op["dsem"]], 16)
                elif op["inc"]:
                    ins.then_inc(csem[e], 1)

        @block.tensor
        def _(eo):
            run("pe", eo)

        @block.scalar
        def _(eo):
            run("act", eo)

        @block.vector
        def _(eo):
            run("dve", eo)

        @block.gpsimd
        def _(eo):
            run("pool", eo)

        @block.sync
        def _(eo):
            run("sp", eo)

        stack.close()
        return {e: (len(self.ops[e]), nwaits[e]) for e in ENGS}


class Rot:
    def __init__(self, items):
        self.items = list(items)
        self.i = 0

    def next(self):
        it = self.items[self.i % len(self.items)]
        self.i += 1
        return it


VL = 34
V_NMIX, V_NFFN, V_NPL, V_GNW, V_QNW, V_KNW, V_CONV = 0, 8, 16, 24, 26, 27, 28
C_RQ, C_RK, C_RV, C_RG, C_FU, C_AQ, C_AK, C_AV, C_CB, C_CC, C_CH = (0, 256, 512, 768, 1024, 1280, 1536, 1664,
                                                                    1792, 2048, 2304)
T_RQ, T_RK, T_RG, T_AQ, T_AK, T_CB, T_CC, T_CH = 0, 2, 4, 6, 8, 9, 11, 13
NFM = 15
CF_ID, CF_T1, CF_T2, CF_IP1, CF_IM, CF_C127, CF_CJ, CF_ONES, CF_BDC, CF_BDS, CF_TW = (
    0, 128, 256, 384, 512, 640, 641, 642, 706, 834, 962)
CB_ID, CB_P32, CB_P16, CB_BO, CB_OND, CB_SA1, CB_SA2, CB_KB = 0, 128, 256, 384, 512, 640, 896, 1152


def _fm_cols():
    cols = []
    for base in (C_RQ, C_RK, C_RG):
        cols.append(np.arange(base, base + 128))
        cols.append(np.arange(base + 128, base + 256))
    cols.append(np.concatenate([np.arange(C_AQ, C_AQ + 64), np.arange(C_AQ + 128, C_AQ + 192)]))
    cols.append(np.concatenate([np.arange(C_AQ + 64, C_AQ + 128), np.arange(C_AQ + 192, C_AQ + 256)]))
    cols.append(np.arange(C_AK, C_AK + 128))
    for base in (C_CB, C_CC, C_CH):
        cols.append(np.arange(base, base + 128))
        cols.append(np.arange(base + 128, base + 256))
    return cols


def _consts(stypes, smax):
    nst = len(stypes)
    cf = np.zeros((128, CF_TW + 256 * nst), np.float64)
    cb = np.zeros((128, CB_KB + 256 * nst), np.float64)
    idx = np.arange(128)
    cf[:, CF_ID:CF_ID + 128] = np.eye(128)
    dif = idx[None, :] - idx[:, None]
    cf[:, CF_T1:CF_T1 + 128] = np.maximum(dif, 0)
    cf[:, CF_T2:CF_T2 + 128] = np.maximum(-dif, 0)
    cf[:, CF_IP1:CF_IP1 + 128] = (idx + 1)[None, :]
    cf[:, CF_IM:CF_IM + 128] = (128 - idx)[None, :]
    cf[:, CF_C127] = 127 - idx
    cf[:, CF_CJ] = idx
    cf[:, CF_ONES:CF_ONES + 64] = 1.0
    c64 = np.cos(2 * np.pi * np.outer(np.arange(64), np.arange(64)) / 64) / 8.0
    s64 = np.sin(2 * np.pi * np.outer(np.arange(64), np.arange(64)) / 64) / 8.0
    for g in range(2):
        cf[g * 64:(g + 1) * 64, CF_BDC + g * 64:CF_BDC + (g + 1) * 64] = c64
        cf[g * 64:(g + 1) * 64, CF_BDS + g * 64:CF_BDS + (g + 1) * 64] = s64
    cb[:, CB_ID:CB_ID + 128] = np.eye(128)
    for p in range(128):
        cb[p, CB_P32 + (p ^ 32)] = 1.0
        cb[p, CB_P16 + (p ^ 16)] = 1.0
    for g in range(2):
        cb[g * 64:(g + 1) * 64, CB_BO + g * 64:CB_BO + (g + 1) * 64] = 1.0 / 64
    cb[:, CB_OND:CB_OND + 128] = 1.0 / 1024
    c128 = np.cos(2 * np.pi * np.outer(idx, idx) / 128) / np.sqrt(128)
    s128 = np.sin(2 * np.pi * np.outer(idx, idx) / 128) / np.sqrt(128)
    cb[:, CB_SA1:CB_SA1 + 128] = c128
    cb[:, CB_SA1 + 128:CB_SA1 + 256] = s128
    cb[:, CB_SA2:CB_SA2 + 128] = -s128
    cb[:, CB_SA2 + 128:CB_SA2 + 256] = c128
    for sti, S in enumerate(stypes):
        nb = S // 128
        fs = 128 // nb
        b = idx // fs
        fp = idx % fs
        ang = 2 * np.pi * np.outer(b, idx) / S
        cf[:, CF_TW + sti * 256:CF_TW + sti * 256 + 128] = np.cos(ang)
        cf[:, CF_TW + sti * 256 + 128:CF_TW + sti * 256 + 256] = np.sin(ang)
        ang2 = 2 * np.pi * np.outer(b, b) / nb
        same = (fp[:, None] == fp[None, :]).astype(np.float64)
        cb[:, CB_KB + sti * 256:CB_KB + sti * 256 + 128] = np.cos(ang2) * same / np.sqrt(nb)
        cb[:, CB_KB + sti * 256 + 128:CB_KB + sti * 256 + 256] = -np.sin(ang2) * same / np.sqrt(nb)
    t = np.arange(smax, dtype=np.float32)
    d = idx % 64
    inv_r = (np.float32(10000.0) ** (-np.arange(32, dtype=np.float32) / np.float32(32))).astype(np.float32)
    ang_r = (t[None, :] * inv_r[d % 32][:, None]).astype(np.float32).astype(np.float64)
    sgn_r = np.where(d < 32, -1.0, 1.0)
    rope_r = np.stack([np.cos(ang_r), np.sin(ang_r) * sgn_r[:, None]])
    inv_a = (np.float32(10000.0) ** (-np.arange(16, dtype=np.float32) / np.float32(16))).astype(np.float32)
    sub = d % 32
    pos = np.where((d < 32)[:, None], (np.arange(smax) // 64)[None, :], (np.arange(smax) % 64)[None, :])
    ang_a = (pos.astype(np.float32) * inv_a[sub % 16][:, None]).astype(np.float32).astype(np.float64)
    sgn_a = np.where(sub < 16, -1.0, 1.0)
    rope_a = np.stack([np.cos(ang_a), np.sin(ang_a) * sgn_a[:, None]])
    return (cf.astype(np.float32), cb.astype(ml_dtypes.bfloat16),
            rope_r.astype(np.float32), rope_a.astype(np.float32))


def _prep_weights(inp, L):
    f = lambda a: np.ascontiguousarray(np.asarray(a, dtype=np.float32))
    w_in = f(inp["w_in"])
    w4 = w_in.reshape(L, 8, 128, -1)
    cols = _fm_cols()
    wfm = np.stack([w4[:, :, :, c] for c in cols], axis=1)
    wfm = np.ascontiguousarray(wfm.transpose(0, 1, 3, 2, 4))
    wrv = np.ascontiguousarray(w4[:, :, :, C_RV:C_RV + 256].transpose(0, 2, 1, 3))
    wav = np.ascontiguousarray(w4[:, :, :, C_AV:C_AV + 128].transpose(0, 2, 1, 3))
    wfut = np.ascontiguousarray(w_in[:, :, C_FU:C_FU + 256].transpose(0, 2, 1).reshape(L, 2, 128, 1024))
    perm = np.concatenate([np.arange(0, 512), 512 + np.arange(0, 64), 512 + np.arange(128, 192),
                           512 + np.arange(64, 128), 512 + np.arange(192, 256), np.arange(768, 1024)])
    w_out = f(inp["w_out"])[:, perm, :]

    def tile_major(w, kin):
        nout = w.shape[2]
        return np.ascontiguousarray(w.reshape(L, kin, 128, nout // 128, 128).transpose(0, 3, 2, 1, 4))

    res = dict(wfm=wfm, wrv=wrv, wav=wav, wfut=wfut,
               wout=tile_major(w_out, 8), w1=tile_major(f(inp["w_ffn1"]), 8),
               w2=tile_major(f(inp["w_ffn2"]), 32), wg=tile_major(f(inp["w_pl_gate"]), 8),
               wpl=tile_major(f(inp["w_pl_proj"]), 2))
    vecs = np.zeros((128, L * VL + 8), np.float32)
    col8 = lambda v: np.asarray(v, np.float32).reshape(8, 128).T
    for l in range(L):
        b = l * VL
        vecs[:, b + V_NMIX:b + V_NMIX + 8] = col8(inp["norm_mix_w"][l])
        vecs[:, b + V_NFFN:b + V_NFFN + 8] = col8(inp["norm_ffn_w"][l])
        vecs[:, b + V_NPL:b + V_NPL + 8] = col8(inp["norm_pl_w"][l])
        vecs[:, b + V_GNW:b + V_GNW + 2] = np.asarray(inp["ret_gn_w"][l], np.float32).reshape(2, 128).T
        vecs[:, b + V_QNW] = np.tile(np.asarray(inp["q_norm_w"][l], np.float32), 2)
        vecs[:, b + V_KNW] = np.tile(np.asarray(inp["k_norm_w"][l], np.float32), 2)
        cw = np.asarray(inp["conv_w"][l], np.float32)
        for tap in range(3):
            for ft in range(2):
                vecs[:, b + V_CONV + tap * 2 + ft] = cw[tap, ft * 128:(ft + 1) * 128]
    vecs[:, L * VL:L * VL + 8] = col8(inp["final_norm_w"])
    res["vecs"] = vecs
    lr = np.asarray(inp["ret_log_rate"], np.float32)
    rates = np.zeros((128, 24 + 1024), np.float32)
    hp = np.arange(128) // 64
    for l in range(L):
        for dr in range(2):
            for h in range(4):
                rates[:, (l * 2 + dr) * 4 + h] = lr[l, dr, h]
            for pr in range(2):
                rates[:, 16 + (l * 2 + dr) * 2 + pr] = lr[l, dr, 2 * pr + hp]
                c0 = 24 + ((l * 2 + dr) * 2 + pr) * 128
                rates[:, c0:c0 + 128] = lr[l, dr, 2 * pr + hp][None, :]
    res["rates"] = rates
    return res


WSPECS = dict(wfm=(NFM, 128, 8, 128), wrv=(128, 8, 256), wav=(128, 8, 128),
              wout=(8, 128, 8, 128), w1=(32, 128, 8, 128), w2=(8, 128, 32, 128),
              wg=(8, 128, 8, 128), wpl=(8, 128, 2, 128))


class Ctx:
    pass


def build(seqs, L, mixers=("ret", "fft", "att", "conv")):
    nc = bass.Bass("TRN2", target_bir_lowering=False)
    stypes = sorted(set(seqs))
    smax = max(seqs)
    nst = len(stypes)

    def dt(name, shape, dtype, kind):
        return nc.dram_tensor(name, list(shape), dtype, kind=kind).ap()

    c = Ctx()
    c.nc, c.L, c.seqs, c.stypes, c.mixers = nc, L, seqs, stypes, mixers
    c.X = [dt("x%d" % i, [S, D], F32, "ExternalInput") for i, S in enumerate(seqs)]
    c.PP = [dt("p%d" % i, [L, S, PLE], F32, "ExternalInput") for i, S in enumerate(seqs)]
    c.Y = [dt("y%d" % i, [S, D], F32, "ExternalOutput") for i, S in enumerate(seqs)]
    c.XS = [dt("xs%d" % i, [D, S], F32, "Internal") for i, S in enumerate(seqs)]
    c.MS = [dt("ms%d" % i, [D, S], BF16, "Internal") for i, S in enumerate(seqs)]
    WF = {n: dt(n + "_f", (L,) + s, F32, "ExternalInput") for n, s in WSPECS.items()}
    c.WB = {n: dt(n + "_b", (L,) + s, BF16, "Internal") for n, s in WSPECS.items()}
    WFUT = dt("wfut", [L, 2, 128, 1024], F32, "ExternalInput")
    c.WAB = dt("wab_b", [L, 2, 128, 8, 256], BF16, "Internal")
    NV = L * VL + 8
    NR = 24 + 1024
    NCF = CF_TW + 256 * nst
    NCB = CB_KB + 256 * nst
    VECS_D = dt("vecs", [128, NV], F32, "ExternalInput")
    RATES_D = dt("rates", [128, NR], F32, "ExternalInput")
    CF_D = dt("cf", [128, NCF], F32, "ExternalInput")
    CB_D = dt("cb", [128, NCB], BF16, "ExternalInput")
    c.ROPER = dt("rope_r", [2, 128, smax], F32, "ExternalInput")
    c.ROPEA = dt("rope_a", [2, 128, smax], F32, "ExternalInput")

    st = contextlib.ExitStack()
    sb = lambda n, s, d: st.enter_context(nc.sbuf_tensor(n, list(s), d))
    c.vecs = vecs = sb("vecs_s", [128, NV], F32)
    c.lg = lg = sb("lg_s", [128, NR], F32)
    c.cf = cf = sb("cf_s", [128, NCF], F32)
    c.cb = cb = sb("cb_s", [128, NCB], BF16)
    c.R1 = R1 = sb("R1", [128, 16384], F32)
    c.R3 = R3 = sb("R3", [128, 16384], F32)
    WR = sb("WR", [128, 4, 4096], BF16)
    ps = [st.enter_context(nc.psum_tensor("ps%d" % i, [128, 512], F32)) for i in range(8)]
    c.p = p = Prog(nc)
    p.dummy = sb("dummy_s", [128, 2], F32)[:, 0:1]
    c.identf = cf[:, CF_ID:CF_ID + 128]
    c.identb = cb[:, CB_ID:CB_ID + 128]
    c.wring = Rot([(WR[:, i, :], "wr%d" % i) for i in range(4)])
    c.psrot = psrot = Rot([(ps[i][:, :], "ps%d" % i) for i in range(8)])
    c.ps = ps

    p.dma("sp", [], ["cf"], cf[:], CF_D[:, :])
    p.dma("sp", [], ["cb"], cb[:], CB_D[:, :])
    p.dma("sp", [], ["vecs"], vecs[:], VECS_D[:, :])
    p.dma("sp", [], ["lg"], lg[:], RATES_D[:, :])
    p.act(["lg"], ["lg"], "activation", out=lg[:], in_=lg[:], func=AF.Exp)
    p.dve(["lg"], ["lg"], "tensor_scalar", out=lg[:], in0=lg[:], scalar1=-1.0, scalar2=None, op0=ALU.mult)
    def cast(names, l):
        for n in names:
            s_ = WSPECS[n]
            src, dst = WF[n][l], c.WB[n][l]
            if n == "w2":
                pat = "c p (a k) j -> (c p a) (k j)"
                src, dst = src.rearrange(pat, a=4), dst.rearrange(pat, a=4)
            elif len(s_) == 4:
                src, dst = src.rearrange("c p k j -> (c p) (k j)"), dst.rearrange("c p k j -> (c p) (k j)")
            else:
                src, dst = src.rearrange("p k j -> p (k j)"), dst.rearrange("p k j -> p (k j)")
            p.dma("pool", [], ["W_%s_%d" % (n, l)], dst, src, grp=1)

    SMALL = ("wfm", "wrv", "wav")
    BIG = ("wout", "w1", "w2", "wg", "wpl")
    cast(SMALL, 0)
    if not ("conv" in mixers and "att" in mixers):
        cast(BIG, 0)
        for l_ in range(1, L):
            cast(SMALL + BIG, l_)
    if "fft" in mixers:
        ar = Arena(R3, 16384)
        wft = ar.f32(1024)
        wab_s = ar.bf(8 * 256).rearrange("p (k j) -> p k j", k=8)
        for l in range(L):
            for jh in range(2):
                p.dma("sp", [], ["wft"], wft, WFUT[l, jh])
                for k in range(8):
                    bank, bk = psrot.next()
                    p.pe(["wft", "cf"], [bk], "matmul", bank[:, 0:256], wft[:, k * 128:(k + 1) * 128],
                         cf[:, CF_BDC:CF_BDC + 256], start=True, stop=True)
                    p.act([bk], ["wab_s"], "activation", out=wab_s[:, k, :], in_=bank[:, 0:256], func=AF.Copy)
                p.dma("sp", ["wab_s"], ["W_wab_%d_%d" % (l, jh)], c.WAB[l, jh], wab_s)
    p.barrier()

    c.store_q = "pool"
    for si, S in enumerate(seqs):
        for l in range(L):
            c.cur_l = l
            phase_a(c, si, l)
            p.barrier()
            bg = (si == 0 and l == 0 and "conv" in mixers and "att" in mixers)
            if bg:
                def bg_cast():
                    cast(BIG, 0)
                    for l_ in range(1, L):
                        cast(SMALL + BIG, l_)
                c.bg_cast = bg_cast
            c.store_q = "sp" if bg else "pool"
            if "conv" in mixers:
                mixer_conv(c, si, l)
            else:
                zero_rows(c, si, 768)
            if "att" in mixers:
                mixer_att(c, si, l)
            else:
                zero_rows(c, si, 512)
            if "ret" in mixers:
                mixer_ret(c, si, l)
            else:
                zero_rows(c, si, 0)
            if "fft" in mixers:
                mixer_fft(c, si, l)
            else:
                zero_rows(c, si, 256)
            p.barrier()
            phase_c(c, si, l)
            p.barrier()
    stats = p.finalize()
    st.close()
    return nc, stats


class Arena:
    def __init__(self, t, n):
        self.t, self.n, self.off = t, n, 0

    def f32(self, n):
        assert self.off + n <= self.n, (self.off, n, self.n)
        v = self.t[:, self.off:self.off + n]
        self.off += n
        return v

    def bf(self, n):
        assert n % 2 == 0
        return self.f32(n // 2).bitcast(BF16)


def wload(c, src_ap, n_elems, use_pat=None, wn=("wfm",), **kw):
    wkeys = [("W_%s_%d" % (n, c.cur_l)) if not n.startswith("W_") else n for n in wn]
    slot, key = c.wring.next()
    flat = slot[:, 0:n_elems]
    if len(src_ap.shape) == 3:
        dst = flat.rearrange("p (a b) -> p a b", a=src_ap.shape[1])
    else:
        dst = flat
    c.p.dma("sp", wkeys, [key], dst, src_ap)
    view = flat.rearrange(use_pat, **kw) if use_pat else flat
    return view, key


def rmsnorm(c, ar, xT, xkeys, wbase, outs, outkeys, nk=8):
    p, cb, vecs = c.p, c.cb, c.vecs
    sq, rs = ar["sq"], ar["rs"]
    bank, bk = c.psrot.next()
    for k in range(nk):
        p.act([xkeys[k]], ["sq%d" % k], "activation", out=sq[:, k, :], in_=xT[:, k, :], func=AF.Square)
        p.pe(["sq%d" % k, "cb"], [bk], "matmul", bank, cb[:, CB_OND:CB_OND + 128], sq[:, k, :], start=(k == 0),
             stop=(k == nk - 1))
    p.act([bk, "eps"], ["rs"], "activation", out=rs, in_=bank, func=AF.Ln, bias=ar["eps"])
    p.act(["rs"], ["rs"], "activation", out=rs, in_=rs, func=AF.Exp, scale=-0.5)
    for k in range(nk):
        p.dve([xkeys[k], "rs", "vecs"], [outkeys[k]], "scalar_tensor_tensor", out=outs[k], in0=xT[:, k, :],
              scalar=vecs[:, wbase + k:wbase + k + 1], in1=rs, op0=ALU.mult, op1=ALU.mult)


def proj_fm(c, bank, bk, wv, wk, hT, c0, n=512):
    for k in range(8):
        c.p.pe([wk], [bk], "matmul", bank[:, 0:n], wv[:, k, :], hT[:, k, c0:c0 + n], start=(k == 0),
               stop=(k == 7))


def xks(xk):
    return [xk + "_%d" % m for m in range(8)]


def xs_tile(c, si, t):
    return c.XS[si][:, t * 512:(t + 1) * 512].rearrange("(k p) t -> p k t", p=128)


def zero_rows(c, si, row0):
    p = c.p
    S = c.seqs[si]
    z = c.R3[:, 16000:16256].bitcast(BF16)
    p.dve([], ["zrows"], "memset", z, 0.0)
    for r in range(row0, row0 + 256, 128):
        for c0 in range(0, S, 512):
            p.dma("pool", ["zrows"], ["MS"], c.MS[si][r:r + 128, c0:c0 + 512], z)


def phase_a(c, si, l):
    p, S = c.p, c.seqs[si]
    NT = S // 512
    vb = l * VL
    a1 = Arena(c.R1, 16384)
    c.hT = hT = a1.bf(8 * S).rearrange("p (k t) -> p k t", k=8)
    a3 = Arena(c.R3, 16384)
    arn = dict(sq=a3.bf(4096).rearrange("p (k t) -> p k t", k=8), rs=a3.f32(512), eps=a3.f32(1))
    p.dve([], ["eps"], "memset", arn["eps"], EPS)
    if l == 0:
        xTs = [(a3.f32(4096).rearrange("p (k t) -> p k t", k=8), "xTa0")]
        xins = [(a3.f32(4096).rearrange("p (s d) -> p s d", s=4), "xin%d" % i) for i in range(2)]
    else:
        xTs = [(a3.f32(4096).rearrange("p (k t) -> p k t", k=8), "xTa%d" % i) for i in range(2)]

    def load(t):
        if l == 0:
            xin, xik = xins[t % 2]
            p.dma("sp", [], [xik], xin, c.X[si][t * 512:(t + 1) * 512, :].rearrange("(s p) d -> p s d", p=128))
        else:
            xT, xb = xTs[t % 2]
            p.dma("sp", ["XS"], xks(xb), xT, xs_tile(c, si, t))

    load(0)
    for t in range(NT):
        if t + 1 < NT:
            load(t + 1)
        xT, xb = xTs[t % len(xTs)]
        xk = xks(xb)
        if l == 0:
            xin, xik = xins[t % 2]
            for k in range(8):
                bank, bk = c.psrot.next()
                for s in range(4):
                    p.pe([xik, "cf"], [bk], "transpose", bank[:, s * 128:(s + 1) * 128],
                         xin[:, s, k * 128:(k + 1) * 128], c.identf)
                if k % 2 == 0:
                    p.act([bk], [xk[k]], "activation", out=xT[:, k, :], in_=bank, func=AF.Copy)
                else:
                    p.dve([bk], [xk[k]], "tensor_copy", out=xT[:, k, :], in_=bank)
            p.dma("pool", xk, ["XS"], xs_tile(c, si, t), xT)
        rmsnorm(c, arn, xT, xk, vb + V_NMIX, [hT[:, k, t * 512:(t + 1) * 512] for k in range(8)],
                ["hT"] * 8)


def mixer_conv(c, si, l):
    p, S, hT, vecs = c.p, c.seqs[si], c.hT, c.vecs
    NT = S // 512
    vb = l * VL
    a3 = Arena(c.R3, 16384)
    zT = a3.bf(2 * (S + 2)).rearrange("p (f t) -> p f t", f=2)
    tmpr = Rot([(a3.f32(512), "tmpc%d" % i) for i in range(2)])
    accr = Rot([(a3.f32(512), "acc%d" % i) for i in range(2)])
    stor = Rot([(a3.bf(512), "sto%d" % i) for i in range(3)])
    p.dve([], ["zT"], "memset", zT[:, :, 0:1], 0.0)
    p.dve([], ["zT"], "memset", zT[:, :, S + 1:S + 2], 0.0)
    wA, kA = wload(c, c.WB["wfm"][l, T_CC:T_CC + 4].rearrange("c p k j -> p c (k j)"), 4096,
                   "p (g k j) -> p g k j", g=4, k=8)
    wBv, kB = wload(c, c.WB["wfm"][l, T_CB:T_CB + 2].rearrange("c p k j -> p c (k j)"), 2048,
                    "p (g k j) -> p g k j", g=2, k=8)
    for t in range(NT):
        for ft in range(2):
            bc, bck = c.psrot.next()
            bh, bhk = c.psrot.next()
            proj_fm(c, bc, bck, wA[:, ft], kA, hT, t * 512)
            proj_fm(c, bh, bhk, wA[:, 2 + ft], kA, hT, t * 512)
            tm, tmk = tmpr.next()
            p.act([bck], [tmk], "activation", out=tm, in_=bc, func=AF.Copy)
            p.dve([tmk, bhk], ["zT"], "tensor_tensor", out=zT[:, ft, 1 + t * 512:1 + (t + 1) * 512], in0=tm,
                  in1=bh, op=ALU.mult)
    for t in range(NT):
        for ft in range(2):
            bb, bbk = c.psrot.next()
            proj_fm(c, bb, bbk, wBv[:, ft], kB, hT, t * 512)
            ac, ack = accr.next()
            so, sok = stor.next()
            c0 = t * 512
            wc = lambda tap: vecs[:, vb + V_CONV + tap * 2 + ft:vb + V_CONV + tap * 2 + ft + 1]
            p.dve(["zT", "vecs"], [ack], "tensor_scalar", out=ac, in0=zT[:, ft, c0 + 1:c0 + 513], scalar1=wc(1),
                  scalar2=None, op0=ALU.mult)
            p.dve(["zT", "vecs", ack], [ack], "scalar_tensor_tensor", out=ac, in0=zT[:, ft, c0:c0 + 512],
                  scalar=wc(0), in1=ac, op0=ALU.mult, op1=ALU.add)
            p.dve(["zT", "vecs", ack], [ack], "scalar_tensor_tensor", out=ac, in0=zT[:, ft, c0 + 2:c0 + 514],
                  scalar=wc(2), in1=ac, op0=ALU.mult, op1=ALU.add)
            p.dve([ack, bbk], [sok], "tensor_tensor", out=so, in0=ac, in1=bb, op=ALU.mult)
            p.dma(c.store_q, [sok], ["MS"], c.MS[si][768 + ft * 128:768 + (ft + 1) * 128, c0:c0 + 512], so)
    p.barrier()


def phase_c(c, si, l):
    p, S, L = c.p, c.seqs[si], c.L
    NT = S // 512
    vb = l * VL
    last = (l == L - 1)
    a1 = Arena(c.R1, 16384)
    hf_raw = a1.f32(8192)
    hfT = hf_raw.bitcast(BF16).rearrange("p (f t) -> p f t", f=32)
    yo = hf_raw[:, 0:4096].rearrange("p (s d) -> p s d", s=4)
    yokeys = ["hfT%d" % f for f in range(16)]
    h2T = a1.bf(8 * 512).rearrange("p (k t) -> p k t", k=8)
    arn = dict(sq=a1.bf(4096).rearrange("p (k t) -> p k t", k=8), rs=a1.f32(512), eps=a1.f32(1))
    rlr = Rot([(a1.bf(512), "rl%d" % i) for i in range(2)])
    gts = Rot([(a1.f32(512), "gt%d" % i) for i in range(2)])
    tms = Rot([(a1.f32(512), "tmg%d" % i) for i in range(2)])
    a3 = Arena(c.R3, 16384)
    xTs = [(a3.f32(4096).rearrange("p (k t) -> p k t", k=8), "xTc%d" % i) for i in range(2)]
    mixs = [(a3.bf(4096).rearrange("p (k t) -> p k t", k=8), "mix%d" % i) for i in range(2)]
    pins = [(a3.f32(1024).rearrange("p (s d) -> p s d", s=4), "pin%d" % i) for i in range(2)]
    pT = a3.bf(1024).rearrange("p (k t) -> p k t", k=2)
    p.dve([], ["eps"], "memset", arn["eps"], EPS)
    WB = c.WB

    def c_loads(t):
        xv, xk = xTs[t % 2]
        mv, mk = mixs[t % 2]
        pv, pk = pins[t % 2]
        p.dma("sp", ["XS"], xks(xk), xv, xs_tile(c, si, t))
        p.dma("sp", ["MS"], [mk], mv, c.MS[si][:, t * 512:(t + 1) * 512].rearrange("(k p) t -> p k t", p=128))
        p.dma("sp", [], [pk], pv, c.PP[si][l, t * 512:(t + 1) * 512, :].rearrange("(s p) d -> p s d", p=128))

    pat4 = dict(use_pat="p (g k j) -> p g k j", g=4, k=8)
    items = []
    for t in range(NT):
        for g in range(2):
            items.append((t, "wout", g, WB["wout"][l, g * 4:(g + 1) * 4].rearrange("c p k j -> p c (k j)"), 4096, pat4))
        for g in range(8):
            items.append((t, "w1", g, WB["w1"][l, g * 4:(g + 1) * 4].rearrange("c p k j -> p c (k j)"), 4096, pat4))
        for m in range(8):
            items.append((t, "w2", m, WB["w2"][l, m].rearrange("p k j -> p (k j)"), 4096,
                          dict(use_pat="p (k j) -> p k j", k=32)))
        items.append((t, "wpl", 0, None, 0, None))
        for g in range(2):
            items.append((t, "wg", g, WB["wg"][l, g * 4:(g + 1) * 4].rearrange("c p k j -> p c (k j)"), 4096, pat4))
    loaded = []

    wpl_flat = a3.bf(2048)
    wplv = (wpl_flat.rearrange("p (g k j) -> p g k j", g=8, k=2), "wplb")
    p.dma("sp", ["W_wpl_%d" % l], ["wplb"], wpl_flat.rearrange("p (a b) -> p a b", a=8),
          WB["wpl"][l].rearrange("c p k j -> p c (k j)"))

    def load_next():
        if len(loaded) < len(items):
            t, nm, g, src, n, kw = items[len(loaded)]
            loaded.append(wload(c, src, n, wn=("wout", "w1", "w2", "wg", "wpl"), **kw) if src is not None else (None, None))

    rsb = (c.ps[7][:, :], "ps7")
    psrot_all = c.psrot
    c.psrot = Rot([(c.ps[i][:, :], "ps%d" % i) for i in range(7)])

    pend = []

    def norm_part(m, xv, xk, wbase, rsb):
        km = xk + "_%d" % m
        p.act([km, "vecs"], ["h2T"], "activation", out=h2T[:, m, :], in_=xv[:, m, :], func=AF.Identity,
              scale=c.vecs[:, wbase + m:wbase + m + 1])
        p.act([km], ["sq%d" % m], "activation", out=arn["sq"][:, m, :], in_=xv[:, m, :], func=AF.Square)
        pend.append(m)

    def flush():
        while pend:
            m = pend.pop(0)
            p.pe(["sq%d" % m, "cb"], [rsb[1]], "matmul", rsb[0], c.cb[:, CB_OND:CB_OND + 128], arn["sq"][:, m, :],
                 start=(m == 0), stop=(m == 7))
            if m == 7:
                p.act([rsb[1], "eps"], ["rs"], "activation", out=arn["rs"], in_=rsb[0], func=AF.Ln, bias=arn["eps"])
                p.act(["rs"], ["rs"], "activation", out=arn["rs"], in_=arn["rs"], func=AF.Exp, scale=-0.5)

    c_loads(0)
    for _ in range(3):
        load_next()
    for ii, (t, nm, g, src, n, kw) in enumerate(items):
        load_next()
        wv, wk = loaded[ii]
        xv, xk = xTs[t % 2]
        mv, mk = mixs[t % 2]
        pv, pk = pins[t % 2]
        if nm == "wout":
            if g == 0 and t + 1 < NT:
                c_loads(t + 1)
            for mi in range(4):
                m = g * 4 + mi
                bank, bk = c.psrot.next()
                for k in range(8):
                    p.pe([wk, mk], [bk], "matmul", bank, wv[:, mi, k, :], mv[:, k, :], start=(k == 0), stop=(k == 7))
                flush()
                p.dve([bk, xk + "_%d" % m], [xk + "_%d" % m], "tensor_tensor", out=xv[:, m, :], in0=bank, in1=xv[:, m, :], op=ALU.add)
                norm_part(m, xv, xk, vb + V_NFFN, rsb)
        elif nm == "w1":
            for fi in range(4):
                f = g * 4 + fi
                bank, bk = c.psrot.next()
                for k in range(8):
                    p.pe([wk, "h2T"], [bk], "matmul", bank, wv[:, fi, k, :], h2T[:, k, :], start=(k == 0),
                         stop=(k == 7))
                flush()
                rv, rk = rlr.next()
                p.dve([bk, "rs"], [rk], "scalar_tensor_tensor", out=rv, in0=bank, scalar=0.0, in1=arn["rs"],
                      op0=ALU.max, op1=ALU.mult)
                p.pool([rk], ["hfT%d" % f], "tensor_tensor", out=hfT[:, f, :], in0=rv, in1=rv, op=ALU.mult)
        elif nm == "w2":
            m = g
            bank, bk = c.psrot.next()
            for f in range(32):
                p.pe([wk, "hfT%d" % f], [bk], "matmul", bank, wv[:, f, :], hfT[:, f, :], start=(f == 0),
                     stop=(f == 31))
            flush()
            p.dve([bk, xk + "_%d" % m], [xk + "_%d" % m], "tensor_tensor", out=xv[:, m, :], in0=bank, in1=xv[:, m, :], op=ALU.add)
            norm_part(m, xv, xk, vb + V_NPL, rsb)
        elif nm == "wpl":
            for k2 in range(2):
                bank, bk = c.psrot.next()
                for s in range(4):
                    p.pe([pk, "cf"], [bk], "transpose", bank[:, s * 128:(s + 1) * 128],
                         pv[:, s, k2 * 128:(k2 + 1) * 128], c.identf)
                p.act([bk], ["pT"], "activation", out=pT[:, k2, :], in_=bank, func=AF.Copy)
            flush()
        elif nm == "wg":
            wpv, wpk = wplv
            for mi in range(4):
                m = g * 4 + mi
                bank, bk = c.psrot.next()
                for k in range(8):
                    p.pe([wk, "h2T"], [bk], "matmul", bank, wv[:, mi, k, :], h2T[:, k, :], start=(k == 0),
                         stop=(k == 7))
                gv, gk = gts.next()
                p.dve([bk, "rs"], [gk], "tensor_tensor", out=gv, in0=bank, in1=arn["rs"], op=ALU.mult)
                p.act([gk], [gk], "activation", out=gv, in_=gv, func=AF.Sigmoid)
                b2, b2k = c.psrot.next()
                for k2 in range(2):
                    p.pe([wpk, "pT"], [b2k], "matmul", b2, wpv[:, m, k2, :], pT[:, k2, :], start=(k2 == 0),
                         stop=(k2 == 1))
                tv, tk = tms.next()
                p.dve([gk, b2k], [tk], "tensor_tensor", out=tv, in0=gv, in1=b2, op=ALU.mult)
                p.pool([tk, xk + "_%d" % m], [xk + "_%d" % m], "tensor_tensor", out=xv[:, m, :], in0=xv[:, m, :], in1=tv, op=ALU.add)
            if g == 1:
                if not last:
                    p.dma("pool", xks(xk), ["XS"], xs_tile(c, si, t), xv)
                else:
                    rmsnorm(c, arn, xv, xks(xk), L * VL, [xv[:, k, :] for k in range(8)], xks(xk))
                    for s in range(4):
                        for hf in range(2):
                            bank, bk = c.psrot.next()
                            for kk in range(4):
                                p.pe(xks(xk) + ["cf"], [bk], "transpose", bank[:, kk * 128:(kk + 1) * 128],
                                     xv[:, hf * 4 + kk, s * 128:(s + 1) * 128], c.identf)
                            if hf == 0:
                                p.act([bk], yokeys, "activation", out=yo[:, s, 0:512], in_=bank, func=AF.Copy)
                            else:
                                p.dve([bk], yokeys, "tensor_copy", out=yo[:, s, 512:1024], in_=bank)
                    p.dma("pool", yokeys, ["Y"],
                          c.Y[si][t * 512:(t + 1) * 512, :].rearrange("(s p) d -> p s d", p=128), yo)
    c.psrot = psrot_all


def mixer_att(c, si, l):
    p, S, hT, vecs, cb, cf = c.p, c.seqs[si], c.hT, c.vecs, c.cb, c.cf
    NT, NCH = S // 512, S // 128
    vb = l * VL
    a3 = Arena(c.R3, 16384)
    QrT = a3.bf(2 * S).rearrange("p (i t) -> p i t", i=2)
    KmT = a3.bf(2 * S).rearrange("p (h t) -> p h t", h=2)
    Vaug = a3.bf(NCH * 2 * 66).rearrange("p (n h d) -> p n h d", n=NCH, h=2)
    eps = a3.f32(1)
    off0 = a3.off
    trigs = [(a3.f32(1024).rearrange("p (a t) -> p a t", a=2), "atrig%d" % i) for i in range(2)]
    sqr = Rot([(a3.bf(512), "asq%d" % i) for i in range(2)])
    qnr = Rot([(a3.bf(512), "aqn%d" % i) for i in range(2)])
    rsr = Rot([(a3.f32(512), "ars%d" % i) for i in range(1)])
    t1r = Rot([(a3.f32(512), "at1%d" % i) for i in range(2)])
    t2r = Rot([(a3.f32(512), "at2%d" % i) for i in range(2)])
    a3.off = off0
    ptr = Rot([(a3.bf(512), "apt%d" % i) for i in range(3)])
    rrr = Rot([(a3.f32(512), "arr%d" % i) for i in range(2)])
    osr = Rot([(a3.f32(512), "aos%d" % i) for i in range(2)])
    ostr = Rot([(a3.bf(512), "aost%d" % i) for i in range(2)])
    p.dve([], ["QK"], "memset", KmT[:, 0, :], 0.0)
    p.pool([], ["QK"], "memset", KmT[:, 1, :], 0.0)
    p.dve([], ["eps"], "memset", eps, EPS)
    p.dve([], ["Vaug"], "memset", Vaug[:, :, :, 64:65], 1.0)
    wqk, kqk = wload(c, c.WB["wfm"][l, T_AQ:T_AQ + 3].rearrange("c p k j -> p c (k j)"), 3072,
                     "p (g k j) -> p g k j", g=3, k=8)
    wv_, kv_ = wload(c, c.WB["wav"][l].rearrange("p k j -> p (k j)"), 1024, "p (k j) -> p k j", wn=("wav",), k=8)
    for kc in range(NCH):
        bank, bk = c.psrot.next()
        for k in range(8):
            p.pe([kv_], [bk], "matmul", bank[:, 0:128], hT[:, k, kc * 128:(kc + 1) * 128], wv_[:, k, :],
                 start=(k == 0), stop=(k == 7))
        src = bank[:, 0:128].rearrange("p (h d) -> p h d", h=2)
        if kc % 2 == 0:
            p.act([bk], ["Vaug"], "activation", out=Vaug[:, kc, :, 0:64], in_=src, func=AF.Copy)
        else:
            p.dve([bk], ["Vaug"], "tensor_copy", out=Vaug[:, kc, :, 0:64], in_=src)
    blocks = [(t, j) for t in range(NT) for j in range(3)]
    pj = {}

    def emit_proj(bi):
        t, j = blocks[bi]
        bank, bk = c.psrot.next()
        proj_fm(c, bank, bk, wqk[:, j], kqk, hT, t * 512)
        pj[bi] = (bank, bk)

    def load_trig(t):
        p.dma("sp", [], [trigs[t % 2][1]], trigs[t % 2][0],
              c.ROPEA[:, :, t * 512:(t + 1) * 512].rearrange("a p t -> p a t"))

    load_trig(0)
    for bi in range(min(2, len(blocks))):
        emit_proj(bi)
    for bi, (t, j) in enumerate(blocks):
        c0 = t * 512
        if j == 0 and t + 1 < NT:
            load_trig(t + 1)
        trig, trk = trigs[t % 2]
        bank, bk = pj.pop(bi)
        sq, sqk = sqr.next()
        p.act([bk], [sqk], "activation", out=sq, in_=bank, func=AF.Square)
        b2, b2k = c.psrot.next()
        p.pe([sqk, "cb"], [b2k], "matmul", b2, cb[:, CB_BO:CB_BO + 128], sq, start=True, stop=True)
        rs, rsk = rsr.next()
        p.act([b2k, "eps"], [rsk], "activation", out=rs, in_=b2, func=AF.Ln, bias=eps)
        p.act([rsk], [rsk], "activation", out=rs, in_=rs, func=AF.Exp, scale=-0.5)
        qn, qnk = qnr.next()
        wcol = vb + (V_QNW if j < 2 else V_KNW)
        p.dve([bk, rsk, "vecs"], [qnk], "scalar_tensor_tensor", out=qn, in0=bank,
              scalar=vecs[:, wcol:wcol + 1], in1=rs, op0=ALU.mult, op1=ALU.mult)
        if bi + 2 < len(blocks):
            emit_proj(bi + 2)
        b3, b3k = c.psrot.next()
        p.pe([qnk, "cb"], [b3k], "matmul", b3, cb[:, CB_P16:CB_P16 + 128], qn, start=True, stop=True)
        t1, t1k = t1r.next()
        t2, t2k = t2r.next()
        p.pool([qnk, trk], [t1k], "tensor_tensor", out=t1, in0=qn, in1=trig[:, 0, :], op=ALU.mult)
        p.dve([b3k, trk], [t2k], "tensor_tensor", out=t2, in0=b3, in1=trig[:, 1, :], op=ALU.mult)
        if j < 2:
            p.dve([t1k, t2k], ["QK"], "tensor_tensor", out=QrT[:, j, c0:c0 + 512], in0=t1, in1=t2, op=ALU.add)
        else:
            for hv in range(2):
                h0 = hv * 64
                p.pool([t1k, t2k], ["QK"], "tensor_tensor", out=KmT[h0:h0 + 64, hv, c0:c0 + 512],
                       in0=t1[h0:h0 + 64, :], in1=t2[h0:h0 + 64, :], op=ALU.add)
    p.barrier()
    if getattr(c, "bg_cast", None) is not None:
        c.bg_cast()
        c.bg_cast = None
    psS = Rot([(c.ps[i][:, :], "ps%d" % i) for i in range(4)])
    psO = Rot([(c.ps[i][:, :], "ps%d" % i) for i in (4, 5, 6, 7)])
    steps = [(qt, i, hv, kc) for qt in range(NT) for i in range(2) for hv in range(2) for kc in range(NCH)]
    deferred = []
    LA = 2

    def emit_scores(j):
        qt, i, hv, kc = steps[j]
        q0 = qt * 512
        h0 = hv * 64
        bs, bsk = psS.next()
        p.pe(["QK"], [bsk], "matmul", bs, KmT[:, hv, kc * 128:(kc + 1) * 128], QrT[:, i, q0:q0 + 512],
             start=True, stop=True)
        return (bs, bsk)

    def fin1(qt, i, hv, bo, bok):
        osb, osk = osr.next()
        p.dve([bok], [osk], "tensor_copy", out=osb[0:65, :], in_=bo[0:65, :])
        rr, rrk = rrr.next()
        p.dve([osk], [rrk], "reciprocal", out=rr[64:65, :], in_=osb[64:65, :])
        return (qt, i, hv, osb, osk, rr, rrk)

    def fin2(qt, i, hv, osb, osk, rr, rrk):
        br, brk = psS.next()
        p.pe([rrk, "cf"], [brk], "matmul", br[0:64, :], cf[64:65, CF_ONES:CF_ONES + 64], rr[64:65, :],
             start=True, stop=True)
        ost, ostk = ostr.next()
        p.dve([osk, brk], [ostk], "tensor_tensor", out=ost[0:64, :], in0=osb[0:64, :], in1=br[0:64, :],
              op=ALU.mult)
        r0 = 512 + i * 128 + hv * 64
        p.dma(c.store_q, [ostk], ["MS"], c.MS[si][r0:r0 + 64, qt * 512:(qt + 1) * 512], ost[0:64, :])

    scq = [emit_scores(j) for j in range(min(LA, len(steps)))]
    cur = None
    for j, (qt, i, hv, kc) in enumerate(steps):
        if j + LA < len(steps):
            scq.append(emit_scores(j + LA))
        bs, bsk = scq[j]
        if kc == 0:
            cur = psO.next()
        pt, ptk = ptr.next()
        p.act([bsk], [ptk], "activation", out=pt, in_=bs, func=AF.Exp, scale=0.125)
        bo, bok = cur
        p.pe([ptk, "Vaug"], [bok], "matmul", bo[0:65, :], Vaug[:, kc, hv, 0:65], pt,
             start=(kc == 0), stop=(kc == NCH - 1))
        for dj, args in list(deferred):
            if dj <= j:
                fin2(*args)
                deferred.remove((dj, args))
        if kc == NCH - 1:
            deferred.append((j + 6, fin1(qt, i, hv, bo, bok)))
    for dj, args in deferred:
        fin2(*args)
    p.barrier()


def mixer_ret(c, si, l):
    p, S, hT, vecs, cb, cf, lg = c.p, c.seqs[si], c.hT, c.vecs, c.cb, c.cf, c.lg
    NT, NCH = S // 512, S // 128
    vb = l * VL
    for pr in range(2):
        a3 = Arena(c.R3, 16384)
        QrT = a3.bf(S)
        KrT = a3.bf(S)
        Vt = a3.bf(NCH * 128).rearrange("p (n d) -> p n d", n=NCH)
        SF = a3.bf(NCH * 64).rearrange("p (n d) -> p n d", n=NCH)
        SB = a3.bf(NCH * 64).rearrange("p (n d) -> p n d", n=NCH)
        trig = a3.f32(1024).rearrange("p (a t) -> p a t", a=2)
        f5 = Rot([(a3.f32(512), "rf5%d" % i) for i in range(4)])
        qbr = Rot([(a3.bf(512), "rqb%d" % i) for i in range(2)])
        DT = a3.f32(256)
        ZF, ZB, XF, XB = a3.f32(128), a3.f32(128), a3.f32(128), a3.f32(128)
        gF, gB, eps = a3.f32(1), a3.f32(1), a3.f32(1)
        kzr = Rot([(a3.bf(128), "rkz%d" % i) for i in range(4)])
        stF, stB, stF2, stB2 = a3.f32(128), a3.f32(128), a3.f32(128), a3.f32(128)
        XBb = a3.bf(128)
        off_ov = a3.off
        ptr = Rot([(a3.bf(256), "rpt%d" % i) for i in range(4)])
        qxr = Rot([(a3.bf(512).rearrange("p (c i) -> p c i", c=4), "rqx%d" % i) for i in range(2)])
        assert a3.off - off_ov == 1024
        trigs = [(trig, "rtrig0"), (c.R3[:, off_ov:off_ov + 1024].rearrange("p (a t) -> p a t", a=2), "rtrig1")]
        obr = Rot([(a3.bf(512), "rob%d" % i) for i in range(2)])
        dsr = Rot([(a3.bf(512), "rds%d" % i) for i in range(1)])
        sgr = Rot([(a3.bf(512), "rsg%d" % i) for i in range(1)])
        ysr = Rot([(a3.bf(512), "rys%d" % i) for i in range(2)])
        p.dve([], ["eps"], "memset", eps, EPS)
        cF = 16 + (l * 2 + 0) * 2 + pr
        cB = 16 + (l * 2 + 1) * 2 + pr
        lgFc, lgBc = lg[:, cF:cF + 1], lg[:, cB:cB + 1]
        rF = 24 + ((l * 2 + 0) * 2 + pr) * 128
        rB = 24 + ((l * 2 + 1) * 2 + pr) * 128
        tb, tbk = f5.next()
        for hh in range(2):
            hF = (l * 2 + 0) * 4 + 2 * pr + hh
            hB = (l * 2 + 1) * 4 + 2 * pr + hh
            tv = tb[:, hh * 128:(hh + 1) * 128]
            p.dve(["cf", "lg"], [tbk], "tensor_scalar", out=tv, in0=cf[:, CF_T1:CF_T1 + 128],
                  scalar1=lg[:, hF:hF + 1], scalar2=None, op0=ALU.mult)
            p.dve(["cf", "lg", tbk], [tbk], "scalar_tensor_tensor", out=tv, in0=cf[:, CF_T2:CF_T2 + 128],
                  scalar=lg[:, hB:hB + 1], in1=tv, op0=ALU.mult, op1=ALU.add)
        p.act([tbk], ["rtab"], "activation", out=DT, in_=tb[:, 0:256], func=AF.Exp)
        p.act(["lg", "cf"], ["rtab"], "activation", out=ZF, in_=lg[:, rF:rF + 128], func=AF.Exp,
              scale=cf[:, CF_C127:CF_C127 + 1])
        p.act(["lg", "cf"], ["rtab"], "activation", out=ZB, in_=lg[:, rB:rB + 128], func=AF.Exp,
              scale=cf[:, CF_CJ:CF_CJ + 1])
        p.act(["lg", "cf"], ["rtab"], "activation", out=XF, in_=cf[:, CF_IP1:CF_IP1 + 128], func=AF.Exp, scale=lgFc)
        p.act(["lg", "cf"], ["rtab"], "activation", out=XB, in_=cf[:, CF_IM:CF_IM + 128], func=AF.Exp, scale=lgBc)
        p.act(["lg", "cf"], ["rtabb"], "activation", out=XBb, in_=cf[:, CF_IM:CF_IM + 128], func=AF.Exp, scale=lgBc)
        p.act(["lg"], ["rtab"], "activation", out=gF, in_=lgFc, func=AF.Exp, scale=128.0)
        p.act(["lg"], ["rtab"], "activation", out=gB, in_=lgBc, func=AF.Exp, scale=128.0)
        wq, kq = wload(c, c.WB["wfm"][l, pr:6:2].rearrange("c p k j -> p c (k j)"), 3072,
                       "p (g k j) -> p g k j", g=3, k=8)
        wv_, kv_ = wload(c, c.WB["wrv"][l][:, :, pr * 128:(pr + 1) * 128], 1024, "p (k j) -> p k j", wn=("wrv",), k=8)
        blocks = [(t, j) for t in range(NT) for j in range(2)]
        pj = {}

        def emit_proj(bi):
            t, j = blocks[bi]
            bank, bk = c.psrot.next()
            proj_fm(c, bank, bk, wq[:, j], kq, hT, t * 512)
            pj[bi] = (bank, bk)

        def load_trig(t):
            p.dma("sp", [], [trigs[t % 2][1]], trigs[t % 2][0],
                  c.ROPER[:, :, t * 512:(t + 1) * 512].rearrange("a p t -> p a t"))

        load_trig(0)
        emit_proj(0)
        for bi, (t, j) in enumerate(blocks):
            c0 = t * 512
            if j == 0 and t + 1 < NT:
                load_trig(t + 1)
            trg, trk = trigs[t % 2]
            bank, bk = pj.pop(bi)
            qb, qbk = qbr.next()
            p.act([bk], [qbk], "activation", out=qb, in_=bank, func=AF.Identity,
                  scale=(1.0 if j == 0 else 0.125))
            if bi + 1 < len(blocks):
                emit_proj(bi + 1)
            b3, b3k = c.psrot.next()
            p.pe([qbk, "cb"], [b3k], "matmul", b3, cb[:, CB_P32:CB_P32 + 128], qb, start=True, stop=True)
            t1, t1k = f5.next()
            t2, t2k = f5.next()
            p.pool([qbk, trk], [t1k], "tensor_tensor", out=t1, in0=qb, in1=trg[:, 0, :], op=ALU.mult)
            p.dve([b3k, trk], [t2k], "tensor_tensor", out=t2, in0=b3, in1=trg[:, 1, :], op=ALU.mult)
            dst = (QrT if j == 0 else KrT)[:, c0:c0 + 512]
            p.dve([t1k, t2k], ["QK"], "tensor_tensor", out=dst, in0=t1, in1=t2, op=ALU.add)
        p.barrier()
        for n in range(NCH):
            bank, bk = c.psrot.next()
            for k in range(8):
                p.pe([kv_], [bk], "matmul", bank[:, 0:128], hT[:, k, n * 128:(n + 1) * 128], wv_[:, k, :],
                     start=(k == 0), stop=(k == 7))
            if n % 2 == 0:
                p.act([bk], ["Vt"], "activation", out=Vt[:, n, :], in_=bank[:, 0:128], func=AF.Copy)
            else:
                p.dve([bk], ["Vt"], "tensor_copy", out=Vt[:, n, :], in_=bank[:, 0:128])
        dirs = ((ZF, gF, SF), (ZB, gB, SB))
        sts = ((stF, stF2), (stB, stB2))
        for dr in range(2):
            p.dve([], ["rst%d_0" % dr], "memset", sts[dr][0], 0.0)
        items = []
        for q in range(NCH):
            items.append((0, q, q))
            items.append((1, NCH - 1 - q, q))
        bTs, kzs = {}, {}

        def emit_T(ii):
            dr, n, q = items[ii]
            bT, bTk = c.psrot.next()
            p.pe(["QK", "cb"], [bTk], "matmul", bT[:, 0:128], KrT[:, n * 128:(n + 1) * 128], c.identb,
                 start=True, stop=True)
            bTs[ii] = (bT, bTk)

        def emit_kz(ii):
            dr, n, q = items[ii]
            bT, bTk = bTs.pop(ii)
            kz, kzk = kzr.next()
            p.dve([bTk, "rtab"], [kzk], "tensor_tensor", out=kz, in0=bT[:, 0:128], in1=dirs[dr][0], op=ALU.mult)
            kzs[ii] = (kz, kzk)

        emit_T(0)
        emit_T(1)
        emit_kz(0)
        for ii, (dr, n, q) in enumerate(items):
            if ii + 2 < len(items):
                emit_T(ii + 2)
            if ii + 1 < len(items):
                emit_kz(ii + 1)
            Z, gcol, Sx = dirs[dr]
            cur, nx = sts[dr][q % 2], sts[dr][(q + 1) % 2]
            ck, nk = "rst%d_%d" % (dr, q % 2), "rst%d_%d" % (dr, (q + 1) % 2)
            kz, kzk = kzs.pop(ii)
            bkv, bkvk = c.psrot.next()
            p.pe([kzk, "Vt"], [bkvk], "matmul", bkv[:, 0:128], kz, Vt[:, n, :], start=True, stop=True)
            p.act([ck], ["rS%d" % dr], "activation", out=Sx[0:64, n, :], in_=cur[0:64, 0:64], func=AF.Copy)
            p.pool([ck], ["rS%d" % dr], "tensor_copy", out=Sx[64:128, n, :], in_=cur[64:128, 64:128])
            p.dve([ck, bkvk, "rtab"], [nk], "scalar_tensor_tensor", out=nx, in0=cur, scalar=gcol,
                  in1=bkv[:, 0:128], op0=ALU.mult, op1=ALU.add)
        def chunk_work(t):
            c0 = t * 512
            qxF, qxFk = qxr.next()
            qxB, qxBk = qxr.next()
            qv = QrT[:, c0:c0 + 512].rearrange("p (c i) -> p c i", c=4)
            for cn in range(4):
                p.dve(["QK", "rtab"], [qxFk], "tensor_tensor", out=qxF[:, cn, :], in0=qv[:, cn, :], in1=XF,
                      op=ALU.mult)
                p.pool(["QK", "rtabb"], [qxBk], "tensor_tensor", out=qxB[:, cn, :], in0=qv[:, cn, :], in1=XBb,
                       op=ALU.mult)
            bOs = [c.psrot.next(), c.psrot.next()]
            bXs = [c.psrot.next(), c.psrot.next()]
            bSs = [c.psrot.next(), c.psrot.next()]
            pts = []
            for cn in range(4):
                n = t * 4 + cn
                cs = slice(n * 128, (n + 1) * 128)
                for hh in range(2):
                    h0 = hh * 64
                    bS, bSk = bSs[hh]
                    p.pe(["QK"], [bSk], "matmul", bS[:, cn * 128:(cn + 1) * 128], KrT[h0:h0 + 64, cs],
                         QrT[h0:h0 + 64, cs], start=True, stop=True)
            for cn in range(4):
                pt, ptk = ptr.next()
                pts.append((pt, ptk))
                for hh in range(2):
                    bS, bSk = bSs[hh]
                    p.dve([bSk, "rtab"], [ptk], "tensor_tensor", out=pt[:, hh * 128:(hh + 1) * 128],
                          in0=bS[:, cn * 128:(cn + 1) * 128], in1=DT[:, hh * 128:(hh + 1) * 128], op=ALU.mult)
            for cn in range(4):
                n = t * 4 + cn
                pt, ptk = pts[cn]
                for hh in range(2):
                    h0 = hh * 64
                    bX, bXk = bXs[hh]
                    xc = bX[h0:h0 + 64, cn * 128:(cn + 1) * 128]
                    bOc, bOck = bOs[cn // 2]
                    p.pe([ptk, "Vt"], [bOck], "matmul", bOc[:, ((cn % 2) * 2 + hh) * 128:((cn % 2) * 2 + hh + 1) * 128],
                         Vt[:, n, :], pt[:, hh * 128:(hh + 1) * 128], start=True, stop=True)
                    p.pe([qxFk, "rS0"], [bXk], "matmul", xc, SF[h0:h0 + 64, n, :], qxF[h0:h0 + 64, cn, :],
                         start=True, stop=False)
                    p.pe([qxBk, "rS1"], [bXk], "matmul", xc, SB[h0:h0 + 64, n, :], qxB[h0:h0 + 64, cn, :],
                         start=False, stop=True)
            oi, oik = f5.next()
            for b2 in range(2):
                bOc, bOck = bOs[b2]
                for hh in range(2):
                    h0 = hh * 64
                    p.act([bOck], [oik], "activation",
                          out=oi[h0:h0 + 64, b2 * 256:(b2 + 1) * 256].rearrange("p (c i) -> p c i", c=2),
                          in_=bOc[h0:h0 + 64, :].rearrange("p (c h i) -> p c h i", c=2, h=2)[:, :, hh, :],
                          func=AF.Copy)
            ob, obk = obr.next()
            for hh in range(2):
                h0 = hh * 64
                p.dve([oik, bXs[hh][1]], [obk], "tensor_tensor", out=ob[h0:h0 + 64, :], in0=oi[h0:h0 + 64, :],
                      in1=bXs[hh][0][h0:h0 + 64, :], op=ALU.add)
            return ob, obk

        def out_chain(t, ob, obk):
            c0 = t * 512
            bM, bMk = c.psrot.next()
            p.pe([obk, "cb"], [bMk], "matmul", bM, cb[:, CB_BO:CB_BO + 128], ob, start=True, stop=True)
            d, dk = f5.next()
            p.dve([obk, bMk], [dk], "tensor_tensor", out=d, in0=ob, in1=bM, op=ALU.subtract)
            ds, dsk = dsr.next()
            p.act([dk], [dsk], "activation", out=ds, in_=d, func=AF.Square)
            bG, bGk = c.psrot.next()
            proj_fm(c, bG, bGk, wq[:, 2], kq, hT, c0)
            sg, sgk = sgr.next()
            p.act([bGk], [sgk], "activation", out=sg, in_=bG, func=AF.Silu)
            bV, bVk = c.psrot.next()
            p.pe([dsk, "cb"], [bVk], "matmul", bV, cb[:, CB_BO:CB_BO + 128], ds, start=True, stop=True)
            rs, rsk = f5.next()
            p.act([bVk, "eps"], [rsk], "activation", out=rs, in_=bV, func=AF.Ln, bias=eps)
            p.act([rsk], [rsk], "activation", out=rs, in_=rs, func=AF.Exp, scale=-0.5)
            p.pool([dk, rsk], [dk], "tensor_tensor", out=d, in0=d, in1=rs, op=ALU.mult)
            ys, ysk = ysr.next()
            p.dve([dk, sgk, "vecs"], [ysk], "scalar_tensor_tensor", out=ys, in0=d,
                  scalar=vecs[:, vb + V_GNW + pr:vb + V_GNW + pr + 1], in1=sg, op0=ALU.mult, op1=ALU.mult)
            p.dma(c.store_q, [ysk], ["MS"], c.MS[si][pr * 128:(pr + 1) * 128, c0:c0 + 512], ys)

        prev = None
        for t in range(NT):
            cw = chunk_work(t)
            if prev is not None:
                out_chain(t - 1, *prev)
            prev = cw
        out_chain(NT - 1, *prev)
        p.barrier()


def mixer_fft(c, si, l):
    p, S, hT, cb, cf = c.p, c.seqs[si], c.hT, c.cb, c.cf
    nb = S // 128
    fs = 128 // nb
    ng = nb
    sti = c.stypes.index(S)
    Tc = cf[:, CF_TW + sti * 256:CF_TW + sti * 256 + 128]
    Ts = cf[:, CF_TW + sti * 256 + 128:CF_TW + sti * 256 + 256]
    KBc = cb[:, CB_KB + sti * 256:CB_KB + sti * 256 + 128]
    KBs = cb[:, CB_KB + sti * 256 + 128:CB_KB + sti * 256 + 256]
    SA1 = cb[:, CB_SA1:CB_SA1 + 256]
    SA2 = cb[:, CB_SA2:CB_SA2 + 256]
    a3 = Arena(c.R3, 16384)
    ABts = [a3.bf(nb * 256).rearrange("p (r g b f) -> p r g b f", r=2, g=ng, b=nb) for _ in range(2)]
    x2r = Rot([(a3.bf(512).rearrange("p (g r c) -> p g r c", g=2, r=2), "fx%d" % i) for i in range(4)])
    Yts = [a3.bf(nb * 128).rearrange("p (d f) -> p d f", d=nb) for _ in range(2)]
    m1r = Rot([(a3.f32(512), "fm1%d" % i) for i in range(2)])
    m2r = Rot([(a3.f32(512), "fm2%d" % i) for i in range(2)])
    stg = Rot([(a3.bf(512), "fst%d" % i) for i in range(2)])
    wabs = [wload(c, c.WAB[l, jh].rearrange("p k j -> p (k j)"), 2048, "p (k j) -> p k j",
                  wn=("W_wab_%d_%d" % (l, jh),), k=8) for jh in range(2)]
    st = {"s1bank": None, "bbank": {}}

    def step1(jh, b):
        wab, kab = wabs[jh]
        ABt = ABts[jh]
        if b % 2 == 0:
            st["s1bank"] = c.psrot.next()
        bank, bk = st["s1bank"]
        reg = bank[:, (b % 2) * 256:(b % 2 + 1) * 256]
        for k in range(8):
            p.pe([kab], [bk], "matmul", reg, hT[:, k, b:S:nb], wab[:, k, :], start=(k == 0), stop=(k == 7))
        for r in range(2):
            src = reg[:, r * 128:(r + 1) * 128].rearrange("p (g f) -> p g f", g=ng)
            if r == 0:
                p.act([bk], ["ABt%d" % jh], "activation", out=ABt[:, r, :, b, :], in_=src, func=AF.Copy)
            else:
                p.dve([bk], ["ABt%d" % jh], "tensor_copy", out=ABt[:, r, :, b, :], in_=src)

    def stage_a(jh, i):
        ABt = ABts[jh]
        bank, bk = c.psrot.next()
        for gl in range(2):
            g = 2 * i + gl
            reg = bank[:, gl * 256:(gl + 1) * 256]
            p.pe(["ABt%d" % jh, "cb"], [bk], "matmul", reg, ABt[:, 0, g].rearrange("p b f -> p (b f)"), SA1,
                 start=True, stop=False)
            p.pe(["ABt%d" % jh, "cb"], [bk], "matmul", reg, ABt[:, 1, g].rearrange("p b f -> p (b f)"), SA2,
                 start=False, stop=True)
        m1, m1k = m1r.next()
        m2, m2k = m2r.next()
        bv = bank.rearrange("p (q c) -> p q c", q=4)
        p.dve([bk, "cf"], [m1k], "tensor_tensor", out=m1.rearrange("p (q c) -> p q c", q=4), in0=bv,
              in1=Tc.unsqueeze(1).broadcast_to([128, 4, 128]), op=ALU.mult)
        p.dve([bk, "cf"], [m2k], "tensor_tensor", out=m2.rearrange("p (q c) -> p q c", q=4), in0=bv,
              in1=Ts.unsqueeze(1).broadcast_to([128, 4, 128]), op=ALU.mult)
        m1v = m1.rearrange("p (g r c) -> p g r c", g=2, r=2)
        m2v = m2.rearrange("p (g r c) -> p g r c", g=2, r=2)
        x2, x2k = x2r.next()
        p.pool([m1k, m2k], [x2k], "tensor_tensor", out=x2[:, :, 0, :], in0=m1v[:, :, 0, :], in1=m2v[:, :, 1, :],
               op=ALU.subtract)
        p.pool([m1k, m2k], [x2k], "tensor_tensor", out=x2[:, :, 1, :], in0=m2v[:, :, 0, :], in1=m1v[:, :, 1, :],
               op=ALU.add)
        return x2, x2k

    def stage_b(jh, i, x2, x2k):
        Yt = Yts[jh]
        if i % 2 == 0:
            st["bbank"][jh] = c.psrot.next()
        bank, bk = st["bbank"][jh]
        for gl in range(2):
            g = 2 * i + gl
            reg = bank[:, (g % 4) * 128:(g % 4 + 1) * 128]
            p.pe([x2k, "cb"], [bk], "matmul", reg, x2[:, gl, 0, :], KBc, start=True, stop=False)
            p.pe([x2k, "cb"], [bk], "matmul", reg, x2[:, gl, 1, :], KBs, start=False, stop=True)
            src = reg.rearrange("p (d f) -> p d f", d=nb)
            if g % 2 == 0:
                p.act([bk], ["Yt%d" % jh], "activation", out=Yt[:, :, g * fs:(g + 1) * fs], in_=src, func=AF.Copy)
            else:
                p.dve([bk], ["Yt%d" % jh], "tensor_copy", out=Yt[:, :, g * fs:(g + 1) * fs], in_=src)

    def transposes(jh, d0):
        Yt = Yts[jh]
        bank, bk = c.psrot.next()
        for dl in range(4):
            p.pe(["Yt%d" % jh, "cb"], [bk], "matmul", bank[:, dl * 128:(dl + 1) * 128], Yt[:, d0 + dl, :], c.identb,
                 start=True, stop=True)
        so, sok = stg.next()
        if (d0 // 4) % 2 == 0:
            p.act([bk], [sok], "activation", out=so, in_=bank, func=AF.Copy)
        else:
            p.dve([bk], [sok], "tensor_copy", out=so, in_=bank)
        p.dma(c.store_q, [sok], ["MS"], c.MS[si][256 + jh * 128:256 + (jh + 1) * 128, d0 * 128:(d0 + 4) * 128], so)

    for b in range(nb):
        step1(0, b)
    pend = None
    for i in range(ng // 2):
        step1(1, 2 * i)
        step1(1, 2 * i + 1)
        x2 = stage_a(0, i)
        if pend is not None:
            stage_b(0, *pend)
        pend = (i,) + x2
    stage_b(0, *pend)
    pend = None
    tdone = 0
    for i in range(ng // 2):
        x2 = stage_a(1, i)
        if pend is not None:
            stage_b(1, *pend)
        pend = (i,) + x2
        if i % 2 == 0 and tdone < nb:
            transposes(0, tdone)
            tdone += 4
    stage_b(1, *pend)
    while tdone < nb:
        transposes(0, tdone)
        tdone += 4
    for d0 in range(0, nb, 4):
        transposes(1, d0)
    p.barrier()


_CACHE = {}


def run(inputs, seqs_per_core, ncores, mixers=("ret", "fft", "att", "conv"), core_seq_arrays=None):
    L = int(np.asarray(inputs["w_in"]).shape[0])
    key = (tuple(seqs_per_core), L, tuple(mixers))
    if key not in _CACHE:
        _CACHE[key] = build(list(seqs_per_core), L, mixers)
    nc, stats = _CACHE[key]
    stypes = sorted(set(seqs_per_core))
    cf, cb, rope_r, rope_a = _consts(stypes, max(seqs_per_core))
    w = _prep_weights(inputs, L)
    shared = {"wfut": w["wfut"], "vecs": w["vecs"], "rates": w["rates"], "cf": cf, "cb": cb,
              "rope_r": rope_r, "rope_a": rope_a}
    for n in WSPECS:
        shared[n + "_f"] = w[n]
    in_maps = []
    for c in range(ncores):
        m = dict(shared)
        for i, (xa, pa) in enumerate(core_seq_arrays[c]):
            m["x%d" % i] = np.ascontiguousarray(xa, dtype=np.float32)
            m["p%d" % i] = np.ascontiguousarray(pa, dtype=np.float32)
        in_maps.append(m)
    res = run_bass_kernel_spmd(nc, in_maps, core_ids=list(range(ncores)))
    return [[np.asarray(res.results[c]["y%d" % i]) for i in range(len(seqs_per_core))] for c in range(ncores)], stats


def kernel(x_prompt, x_sample, p_prompt, p_sample, norm_mix_w, w_in, ret_log_rate, ret_gn_w, q_norm_w, k_norm_w,
           conv_w, w_out, norm_ffn_w, w_ffn1, w_ffn2, norm_pl_w, w_pl_gate, w_pl_proj, final_norm_w):
    inputs = dict(norm_mix_w=norm_mix_w, w_in=w_in, ret_log_rate=ret_log_rate, ret_gn_w=ret_gn_w,
                  q_norm_w=q_norm_w, k_norm_w=k_norm_w, conv_w=conv_w, w_out=w_out, norm_ffn_w=norm_ffn_w,
                  w_ffn1=w_ffn1, w_ffn2=w_ffn2, norm_pl_w=norm_pl_w, w_pl_gate=w_pl_gate, w_pl_proj=w_pl_proj,
                  final_norm_w=final_norm_w)
    inputs = {k: np.asarray(v) for k, v in inputs.items()}
    x_prompt = np.asarray(x_prompt)
    x_sample = np.asarray(x_sample)
    p_prompt = np.asarray(p_prompt)
    p_sample = np.asarray(p_sample)
    B, SP, _ = x_prompt.shape
    BS, SS, _ = x_sample.shape
    npc = B // NCORES
    nsc = BS // NCORES
    seqs = [SP] * npc + [SS] * nsc
    arrs = []
    for c in range(NCORES):
        a = []
        for j in range(npc):
            b = c * npc + j
            a.append((x_prompt[b], p_prompt[:, b]))
        for j in range(nsc):
            b = c * nsc + j
            a.append((x_sample[b], p_sample[:, b]))
        arrs.append(a)
    outs, _ = run(inputs, seqs, NCORES, core_seq_arrays=arrs)
    y_prompt = np.empty((B, SP, D), np.float32)
    y_sample = np.empty((BS, SS, D), np.float32)
    for c in range(NCORES):
        for j in range(npc):
            y_prompt[c * npc + j] = outs[c][j]
        for j in range(nsc):
            y_sample[c * nsc + j] = outs[c][npc + j]
    return (y_prompt, y_sample)
```

```python
import contextlib
import numpy as np
import ml_dtypes
import concourse.bass as bass
import concourse.mybir as mybir
from concourse.bass_utils import run_bass_kernel_spmd

F32 = mybir.dt.float32
BF16 = mybir.dt.bfloat16
AF = mybir.ActivationFunctionType
ALU = mybir.AluOpType

D = 1024
DFF = 4096
PLE = 256
EPS = 1e-6
NCORES = 8
ENGS = ("pe", "act", "dve", "pool", "sp")
N_DMA_SEMS = 24


class Prog:
    def __init__(self, nc):
        self.nc = nc
        self.ops = {e: [] for e in ENGS}
        self.last_w = {}
        self.readers = {}
        self.ndma = {e: 0 for e in ENGS}
        self.last_dma = {}

    def emit(self, eng, fn, reads=(), writes=(), dma=False, extra_deps=(), grp=0):
        ops = self.ops[eng]
        idx = len(ops)
        deps = {}

        def add(d):
            e, i = d
            pd = self.ops[e][i]["dma"]
            if pd:
                key = (e, "d", self.ops[e][i]["dsem"])
            else:
                key = (e, "c")
            if deps.get(key, (None, -1))[1] < i:
                deps[key] = (e, i)

        for k in reads:
            lw = self.last_w.get(k)
            if lw is not None:
                if lw[0] == eng and eng == "pe" and not dma and not self.ops[lw[0]][lw[1]]["dma"]:
                    continue
                add(lw)
        for k in writes:
            lw = self.last_w.get(k)
            if lw is not None:
                if not (lw[0] == eng and not dma and not self.ops[lw[0]][lw[1]]["dma"]):
                    add(lw)
            for e, i in self.readers.get(k, {}).items():
                if e == eng and not dma and not self.ops[e][i]["dma"]:
                    continue
                add((e, i))
        for d in extra_deps:
            add(d)
        op = dict(fn=fn, deps=list(deps.values()), dma=dma, inc=False)
        if dma:
            cnt = self.ndma.get((eng, grp), 0)
            op["dsem"] = grp * N_DMA_SEMS + cnt % N_DMA_SEMS
            self.ndma[(eng, grp)] = cnt + 1
            self.last_dma[(eng, op["dsem"])] = idx
        ops.append(op)
        for k in writes:
            self.last_w[k] = (eng, idx)
            self.readers[k] = {}
        for k in reads:
            self.readers.setdefault(k, {})[eng] = idx
        return (eng, idx)

    @staticmethod
    def _mk(meth, a, kw):
        return lambda eo: getattr(eo, meth)(*a, **kw)

    def pe(self, reads, writes, meth, *a, **kw):
        return self.emit("pe", self._mk(meth, a, kw), reads, writes)

    def act(self, reads, writes, meth, *a, **kw):
        return self.emit("act", self._mk(meth, a, kw), reads, writes)

    def dve(self, reads, writes, meth, *a, **kw):
        return self.emit("dve", self._mk(meth, a, kw), reads, writes)

    def pool(self, reads, writes, meth, *a, **kw):
        return self.emit("pool", self._mk(meth, a, kw), reads, writes)

    def dma(self, q, reads, writes, out, in_, grp=0):
        return self.emit(q, self._mk("dma_start", (), dict(out=out, in_=in_)), reads, writes, dma=True, grp=grp)

    def barrier(self):
        marks = []
        for e in ENGS:
            if e == "pool" and getattr(self, "dummy", None) is not None:
                marks.append(self.emit(e, self._mk("memset", (self.dummy, 0.0), {})))
            else:
                marks.append(self.emit(e, lambda eo: eo.drain()))
        dmas = [(e, i) for (e, s_), i in self.last_dma.items() if s_ < N_DMA_SEMS]
        for e in ENGS:
            self.emit(e, None, extra_deps=[m for m in marks if m[0] != e] + dmas)
        self.last_w = {k: v for k, v in self.last_w.items() if k.startswith("W_")}
        self.readers = {}

    def finalize(self):
        nc = self.nc
        dmas = [(e, i) for (e, _s), i in self.last_dma.items()]
        self.emit("sp", None, extra_deps=dmas)
        for e in ENGS:
            for op in self.ops[e]:
                for pe_, pi in op["deps"]:
                    self.ops[pe_][pi]["inc"] = True
        for e in ENGS:
            c = 0
            dcount = [0] * (2 * N_DMA_SEMS)
            for op in self.ops[e]:
                if op["dma"]:
                    dcount[op["dsem"]] += 16
                    op["dval"] = dcount[op["dsem"]]
                elif op["inc"]:
                    c += 1
                    op["cval"] = c
        stack = contextlib.ExitStack()
        csem = {e: stack.enter_context(nc.semaphore("c_" + e)) for e in ENGS}
        dsem = {e: [stack.enter_context(nc.semaphore("d_%s%d" % (e, i))) for i in range(2 * N_DMA_SEMS)]
                for e in ("sp", "pool")}
        block = stack.enter_context(nc.Block())
        ops_all = self.ops
        nwaits = {e: 0 for e in ENGS}

        def run(e, eo):
            waited = {}
            for op in ops_all[e]:
                for pe_, pi in op["deps"]:
                    pop = ops_all[pe_][pi]
                    if pop["dma"]:
                        s = dsem[pe_][pop["dsem"]]
                        v = pop["dval"]
                        key = ("d", pe_, pop["dsem"])
                    else:
                        s = csem[pe_]
                        v = pop["cval"]
                        key = ("c", pe_)
                    if waited.get(key, 0) >= v:
                        continue
                    waited[key] = v
                    eo.wait_ge(s, v)
                    nwaits[e] += 1
                if op["fn"] is None:
                    continue
                ins = op["fn"](eo)
                if op["dma"]:
                    ins.then_inc(dsem[e][op["dsem"]], 16)
                elif op["inc"]:
                    ins.then_inc(csem[e], 1)

        @block.tensor
        def _(eo):
            run("pe", eo)

        @block.scalar
        def _(eo):
            run("act", eo)

        @block.vector
        def _(eo):
            run("dve", eo)

        @block.gpsimd
        def _(eo):
            run("pool", eo)

        @block.sync
        def _(eo):
            run("sp", eo)

        stack.close()
        return {e: (len(self.ops[e]), nwaits[e]) for e in ENGS}


class Rot:
    def __init__(self, items):
        self.items = list(items)
        self.i = 0

    def next(self):
        it = self.items[self.i % len(self.items)]
        self.i += 1
        return it


VL = 34
V_NMIX, V_NFFN, V_NPL, V_GNW, V_QNW, V_KNW, V_CONV = 0, 8, 16, 24, 26, 27, 28
C_RQ, C_RK, C_RV, C_RG, C_FU, C_AQ, C_AK, C_AV, C_CB, C_CC, C_CH = (0, 256, 512, 768, 1024, 1280, 1536, 1664,
                                                                    1792, 2048, 2304)
T_RQ, T_RK, T_RG, T_AQ, T_AK, T_CB, T_CC, T_CH = 0, 2, 4, 6, 8, 9, 11, 13
NFM = 15
CF_ID, CF_T1, CF_T2, CF_IP1, CF_IM, CF_C127, CF_CJ, CF_ONES, CF_BDC, CF_BDS, CF_TW = (
    0, 128, 256, 384, 512, 640, 641, 642, 706, 834, 962)
CB_ID, CB_P32, CB_P16, CB_BO, CB_OND, CB_SA1, CB_SA2, CB_KB = 0, 128, 256, 384, 512, 640, 896, 1152


def _fm_cols():
    cols = []
    for base in (C_RQ, C_RK, C_RG):
        cols.append(np.arange(base, base + 128))
        cols.append(np.arange(base + 128, base + 256))
    cols.append(np.concatenate([np.arange(C_AQ, C_AQ + 64), np.arange(C_AQ + 128, C_AQ + 192)]))
    cols.append(np.concatenate([np.arange(C_AQ + 64, C_AQ + 128), np.arange(C_AQ + 192, C_AQ + 256)]))
    cols.append(np.arange(C_AK, C_AK + 128))
    for base in (C_CB, C_CC, C_CH):
        cols.append(np.arange(base, base + 128))
        cols.append(np.arange(base + 128, base + 256))
    return cols


def _consts(stypes, smax):
    nst = len(stypes)
    cf = np.zeros((128, CF_TW + 256 * nst), np.float64)
    cb = np.zeros((128, CB_KB + 256 * nst), np.float64)
    idx = np.arange(128)
    cf[:, CF_ID:CF_ID + 128] = np.eye(128)
    dif = idx[None, :] - idx[:, None]
    cf[:, CF_T1:CF_T1 + 128] = np.maximum(dif, 0)
    cf[:, CF_T2:CF_T2 + 128] = np.maximum(-dif, 0)
    cf[:, CF_IP1:CF_IP1 + 128] = (idx + 1)[None, :]
    cf[:, CF_IM:CF_IM + 128] = (128 - idx)[None, :]
    cf[:, CF_C127] = 127 - idx
    cf[:, CF_CJ] = idx
    cf[:, CF_ONES:CF_ONES + 64] = 1.0
    c64 = np.cos(2 * np.pi * np.outer(np.arange(64), np.arange(64)) / 64) / 8.0
    s64 = np.sin(2 * np.pi * np.outer(np.arange(64), np.arange(64)) / 64) / 8.0
    for g in range(2):
        cf[g * 64:(g + 1) * 64, CF_BDC + g * 64:CF_BDC + (g + 1) * 64] = c64
        cf[g * 64:(g + 1) * 64, CF_BDS + g * 64:CF_BDS + (g + 1) * 64] = s64
    cb[:, CB_ID:CB_ID + 128] = np.eye(128)
    for p in range(128):
        cb[p, CB_P32 + (p ^ 32)] = 1.0
        cb[p, CB_P16 + (p ^ 16)] = 1.0
    for g in range(2):
        cb[g * 64:(g + 1) * 64, CB_BO + g * 64:CB_BO + (g + 1) * 64] = 1.0 / 64
    cb[:, CB_OND:CB_OND + 128] = 1.0 / 1024
    c128 = np.cos(2 * np.pi * np.outer(idx, idx) / 128) / np.sqrt(128)
    s128 = np.sin(2 * np.pi * np.outer(idx, idx) / 128) / np.sqrt(128)
    cb[:, CB_SA1:CB_SA1 + 128] = c128
    cb[:, CB_SA1 + 128:CB_SA1 + 256] = s128
    cb[:, CB_SA2:CB_SA2 + 128] = -s128
    cb[:, CB_SA2 + 128:CB_SA2 + 256] = c128
    for sti, S in enumerate(stypes):
        nb = S // 128
        fs = 128 // nb
        b = idx // fs
        fp = idx % fs
        ang = 2 * np.pi * np.outer(b, idx) / S
        cf[:, CF_TW + sti * 256:CF_TW + sti * 256 + 128] = np.cos(ang)
        cf[:, CF_TW + sti * 256 + 128:CF_TW + sti * 256 + 256] = np.sin(ang)
        ang2 = 2 * np.pi * np.outer(b, b) / nb
        same = (fp[:, None] == fp[None, :]).astype(np.float64)
        cb[:, CB_KB + sti * 256:CB_KB + sti * 256 + 128] = np.cos(ang2) * same / np.sqrt(nb)
        cb[:, CB_KB + sti * 256 + 128:CB_KB + sti * 256 + 256] = -np.sin(ang2) * same / np.sqrt(nb)
    t = np.arange(smax, dtype=np.float32)
    d = idx % 64
    inv_r = (np.float32(10000.0) ** (-np.arange(32, dtype=np.float32) / np.float32(32))).astype(np.float32)
    ang_r = (t[None, :] * inv_r[d % 32][:, None]).astype(np.float32).astype(np.float64)
    sgn_r = np.where(d < 32, -1.0, 1.0)
    rope_r = np.stack([np.cos(ang_r), np.sin(ang_r) * sgn_r[:, None]])
    inv_a = (np.float32(10000.0) ** (-np.arange(16, dtype=np.float32) / np.float32(16))).astype(np.float32)
    sub = d % 32
    pos = np.where((d < 32)[:, None], (np.arange(smax) // 64)[None, :], (np.arange(smax) % 64)[None, :])
    ang_a = (pos.astype(np.float32) * inv_a[sub % 16][:, None]).astype(np.float32).astype(np.float64)
    sgn_a = np.where(sub < 16, -1.0, 1.0)
    rope_a = np.stack([np.cos(ang_a), np.sin(ang_a) * sgn_a[:, None]])
    return (cf.astype(np.float32), cb.astype(ml_dtypes.bfloat16),
            rope_r.astype(np.float32), rope_a.astype(np.float32))


def _prep_weights(inp, L):
    f = lambda a: np.ascontiguousarray(np.asarray(a, dtype=np.float32))
    w_in = f(inp["w_in"])
    w4 = w_in.reshape(L, 8, 128, -1)
    cols = _fm_cols()
    wfm = np.stack([w4[:, :, :, c] for c in cols], axis=1)
    wfm = np.ascontiguousarray(wfm.transpose(0, 1, 3, 2, 4))
    wrv = np.ascontiguousarray(w4[:, :, :, C_RV:C_RV + 256].transpose(0, 2, 1, 3))
    wav = np.ascontiguousarray(w4[:, :, :, C_AV:C_AV + 128].transpose(0, 2, 1, 3))
    wfut = np.ascontiguousarray(w_in[:, :, C_FU:C_FU + 256].transpose(0, 2, 1).reshape(L, 2, 128, 1024))
    perm = np.concatenate([np.arange(0, 512), 512 + np.arange(0, 64), 512 + np.arange(128, 192),
                           512 + np.arange(64, 128), 512 + np.arange(192, 256), np.arange(768, 1024)])
    w_out = f(inp["w_out"])[:, perm, :]

    def tile_major(w, kin):
        nout = w.shape[2]
        return np.ascontiguousarray(w.reshape(L, kin, 128, nout // 128, 128).transpose(0, 3, 2, 1, 4))

    res = dict(wfm=wfm, wrv=wrv, wav=wav, wfut=wfut,
               wout=tile_major(w_out, 8), w1=tile_major(f(inp["w_ffn1"]), 8),
               w2=tile_major(f(inp["w_ffn2"]), 32), wg=tile_major(f(inp["w_pl_gate"]), 8),
               wpl=tile_major(f(inp["w_pl_proj"]), 2))
    vecs = np.zeros((128, L * VL + 8), np.float32)
    col8 = lambda v: np.asarray(v, np.float32).reshape(8, 128).T
    for l in range(L):
        b = l * VL
        vecs[:, b + V_NMIX:b + V_NMIX + 8] = col8(inp["norm_mix_w"][l])
        vecs[:, b + V_NFFN:b + V_NFFN + 8] = col8(inp["norm_ffn_w"][l])
        vecs[:, b + V_NPL:b + V_NPL + 8] = col8(inp["norm_pl_w"][l])
        vecs[:, b + V_GNW:b + V_GNW + 2] = np.asarray(inp["ret_gn_w"][l], np.float32).reshape(2, 128).T
        vecs[:, b + V_QNW] = np.tile(np.asarray(inp["q_norm_w"][l], np.float32), 2)
        vecs[:, b + V_KNW] = np.tile(np.asarray(inp["k_norm_w"][l], np.float32), 2)
        cw = np.asarray(inp["conv_w"][l], np.float32)
        for tap in range(3):
            for ft in range(2):
                vecs[:, b + V_CONV + tap * 2 + ft] = cw[tap, ft * 128:(ft + 1) * 128]
    vecs[:, L * VL:L * VL + 8] = col8(inp["final_norm_w"])
    res["vecs"] = vecs
    lr = np.asarray(inp["ret_log_rate"], np.float32)
    rates = np.zeros((128, 24 + 1024), np.float32)
    hp = np.arange(128) // 64
    for l in range(L):
        for dr in range(2):
            for h in range(4):
                rates[:, (l * 2 + dr) * 4 + h] = lr[l, dr, h]
            for pr in range(2):
                rates[:, 16 + (l * 2 + dr) * 2 + pr] = lr[l, dr, 2 * pr + hp]
                c0 = 24 + ((l * 2 + dr) * 2 + pr) * 128
                rates[:, c0:c0 + 128] = lr[l, dr, 2 * pr + hp][None, :]
    res["rates"] = rates
    return res


WSPECS = dict(wfm=(NFM, 128, 8, 128), wrv=(128, 8, 256), wav=(128, 8, 128),
              wout=(8, 128, 8, 128), w1=(32, 128, 8, 128), w2=(8, 128, 32, 128),
              wg=(8, 128, 8, 128), wpl=(8, 128, 2, 128))


class Ctx:
    pass


def build(seqs, L, mixers=("ret", "fft", "att", "conv")):
    nc = bass.Bass("TRN2", target_bir_lowering=False)
    stypes = sorted(set(seqs))
    smax = max(seqs)
    nst = len(stypes)

    def dt(name, shape, dtype, kind):
        return nc.dram_tensor(name, list(shape), dtype, kind=kind).ap()

    c = Ctx()
    c.nc, c.L, c.seqs, c.stypes, c.mixers = nc, L, seqs, stypes, mixers
    c.X = [dt("x%d" % i, [S, D], F32, "ExternalInput") for i, S in enumerate(seqs)]
    c.PP = [dt("p%d" % i, [L, S, PLE], F32, "ExternalInput") for i, S in enumerate(seqs)]
    c.Y = [dt("y%d" % i, [S, D], F32, "ExternalOutput") for i, S in enumerate(seqs)]
    c.XS = [dt("xs%d" % i, [D, S], F32, "Internal") for i, S in enumerate(seqs)]
    c.MS = [dt("ms%d" % i, [D, S], BF16, "Internal") for i, S in enumerate(seqs)]
    WF = {n: dt(n + "_f", (L,) + s, F32, "ExternalInput") for n, s in WSPECS.items()}
    c.WB = {n: dt(n + "_b", (L,) + s, BF16, "Internal") for n, s in WSPECS.items()}
    WFUT = dt("wfut", [L, 2, 128, 1024], F32, "ExternalInput")
    c.WAB = dt("wab_b", [L, 2, 128, 8, 256], BF16, "Internal")
    NV = L * VL + 8
    NR = 24 + 1024
    NCF = CF_TW + 256 * nst
    NCB = CB_KB + 256 * nst
    VECS_D = dt("vecs", [128, NV], F32, "ExternalInput")
    RATES_D = dt("rates", [128, NR], F32, "ExternalInput")
    CF_D = dt("cf", [128, NCF], F32, "ExternalInput")
    CB_D = dt("cb", [128, NCB], BF16, "ExternalInput")
    c.ROPER = dt("rope_r", [2, 128, smax], F32, "ExternalInput")
    c.ROPEA = dt("rope_a", [2, 128, smax], F32, "ExternalInput")

    st = contextlib.ExitStack()
    sb = lambda n, s, d: st.enter_context(nc.sbuf_tensor(n, list(s), d))
    c.vecs = vecs = sb("vecs_s", [128, NV], F32)
    c.lg = lg = sb("lg_s", [128, NR], F32)
    c.cf = cf = sb("cf_s", [128, NCF], F32)
    c.cb = cb = sb("cb_s", [128, NCB], BF16)
    c.R1 = R1 = sb("R1", [128, 16384], F32)
    c.R3 = R3 = sb("R3", [128, 16384], F32)
    WR = sb("WR", [128, 4, 4096], BF16)
    ps = [st.enter_context(nc.psum_tensor("ps%d" % i, [128, 512], F32)) for i in range(8)]
    c.p = p = Prog(nc)
    p.dummy = sb("dummy_s", [128, 2], F32)[:, 0:1]
    c.identf = cf[:, CF_ID:CF_ID + 128]
    c.identb = cb[:, CB_ID:CB_ID + 128]
    c.wring = Rot([(WR[:, i, :], "wr%d" % i) for i in range(4)])
    c.psrot = psrot = Rot([(ps[i][:, :], "ps%d" % i) for i in range(8)])
    c.ps = ps

    p.dma("sp", [], ["cf"], cf[:], CF_D[:, :])
    p.dma("sp", [], ["cb"], cb[:], CB_D[:, :])
    p.dma("sp", [], ["vecs"], vecs[:], VECS_D[:, :])
    p.dma("sp", [], ["lg"], lg[:], RATES_D[:, :])
    p.act(["lg"], ["lg"], "activation", out=lg[:], in_=lg[:], func=AF.Exp)
    p.dve(["lg"], ["lg"], "tensor_scalar", out=lg[:], in0=lg[:], scalar1=-1.0, scalar2=None, op0=ALU.mult)
    def cast(names, l):
        for n in names:
            s_ = WSPECS[n]
            src, dst = WF[n][l], c.WB[n][l]
            if n == "w2":
                pat = "c p (a k) j -> (c p a) (k j)"
                src, dst = src.rearrange(pat, a=4), dst.rearrange(pat, a=4)
            elif len(s_) == 4:
                src, dst = src.rearrange("c p k j -> (c p) (k j)"), dst.rearrange("c p k j -> (c p) (k j)")
            else:
                src, dst = src.rearrange("p k j -> p (k j)"), dst.rearrange("p k j -> p (k j)")
            p.dma("pool", [], ["W_%s_%d" % (n, l)], dst, src, grp=1)

    SMALL = ("wfm", "wrv", "wav")
    BIG = ("wout", "w1", "w2", "wg", "wpl")
    cast(SMALL, 0)
    if not ("conv" in mixers and "att" in mixers):
        cast(BIG, 0)
        for l_ in range(1, L):
            cast(SMALL + BIG, l_)
    if "fft" in mixers:
        ar = Arena(R3, 16384)
        wft = ar.f32(1024)
        wab_s = ar.bf(8 * 256).rearrange("p (k j) -> p k j", k=8)
        for l in range(L):
            for jh in range(2):
                p.dma("sp", [], ["wft"], wft, WFUT[l, jh])
                for k in range(8):
                    bank, bk = psrot.next()
                    p.pe(["wft", "cf"], [bk], "matmul", bank[:, 0:256], wft[:, k * 128:(k + 1) * 128],
                         cf[:, CF_BDC:CF_BDC + 256], start=True, stop=True)
                    p.act([bk], ["wab_s"], "activation", out=wab_s[:, k, :], in_=bank[:, 0:256], func=AF.Copy)
                p.dma("sp", ["wab_s"], ["W_wab_%d_%d" % (l, jh)], c.WAB[l, jh], wab_s)
    p.barrier()

    c.store_q = "pool"
    for si, S in enumerate(seqs):
        for l in range(L):
            c.cur_l = l
            phase_a(c, si, l)
            p.barrier()
            bg = (si == 0 and l == 0 and "conv" in mixers and "att" in mixers)
            if bg:
                def bg_cast():
                    cast(BIG, 0)
                    for l_ in range(1, L):
                        cast(SMALL + BIG, l_)
                c.bg_cast = bg_cast
            c.store_q = "sp" if bg else "pool"
            if "conv" in mixers:
                mixer_conv(c, si, l)
            else:
                zero_rows(c, si, 768)
            if "att" in mixers:
                mixer_att(c, si, l)
            else:
                zero_rows(c, si, 512)
            if "ret" in mixers:
                mixer_ret(c, si, l)
            else:
                zero_rows(c, si, 0)
            if "fft" in mixers:
                mixer_fft(c, si, l)
            else:
                zero_rows(c, si, 256)
            p.barrier()
            phase_c(c, si, l)
            p.barrier()
    stats = p.finalize()
    st.close()
    return nc, stats


class Arena:
    def __init__(self, t, n):
        self.t, self.n, self.off = t, n, 0

    def f32(self, n):
        assert self.off + n <= self.n, (self.off, n, self.n)
        v = self.t[:, self.off:self.off + n]
        self.off += n
        return v

    def bf(self, n):
        assert n % 2 == 0
        return self.f32(n // 2).bitcast(BF16)


def wload(c, src_ap, n_elems, use_pat=None, wn=("wfm",), **kw):
    wkeys = [("W_%s_%d" % (n, c.cur_l)) if not n.startswith("W_") else n for n in wn]
    slot, key = c.wring.next()
    flat = slot[:, 0:n_elems]
    if len(src_ap.shape) == 3:
        dst = flat.rearrange("p (a b) -> p a b", a=src_ap.shape[1])
    else:
        dst = flat
    c.p.dma("sp", wkeys, [key], dst, src_ap)
    view = flat.rearrange(use_pat, **kw) if use_pat else flat
    return view, key


def rmsnorm(c, ar, xT, xkeys, wbase, outs, outkeys, nk=8):
    p, cb, vecs = c.p, c.cb, c.vecs
    sq, rs = ar["sq"], ar["rs"]
    bank, bk = c.psrot.next()
    for k in range(nk):
        p.act([xkeys[k]], ["sq%d" % k], "activation", out=sq[:, k, :], in_=xT[:, k, :], func=AF.Square)
        p.pe(["sq%d" % k, "cb"], [bk], "matmul", bank, cb[:, CB_OND:CB_OND + 128], sq[:, k, :], start=(k == 0),
             stop=(k == nk - 1))
    p.act([bk, "eps"], ["rs"], "activation", out=rs, in_=bank, func=AF.Ln, bias=ar["eps"])
    p.act(["rs"], ["rs"], "activation", out=rs, in_=rs, func=AF.Exp, scale=-0.5)
    for k in range(nk):
        p.dve([xkeys[k], "rs", "vecs"], [outkeys[k]], "scalar_tensor_tensor", out=outs[k], in0=xT[:, k, :],
              scalar=vecs[:, wbase + k:wbase + k + 1], in1=rs, op0=ALU.mult, op1=ALU.mult)


def proj_fm(c, bank, bk, wv, wk, hT, c0, n=512):
    for k in range(8):
        c.p.pe([wk], [bk], "matmul", bank[:, 0:n], wv[:, k, :], hT[:, k, c0:c0 + n], start=(k == 0),
               stop=(k == 7))


def xks(xk):
    return [xk + "_%d" % m for m in range(8)]


def xs_tile(c, si, t):
    return c.XS[si][:, t * 512:(t + 1) * 512].rearrange("(k p) t -> p k t", p=128)


def zero_rows(c, si, row0):
    p = c.p
    S = c.seqs[si]
    z = c.R3[:, 16000:16256].bitcast(BF16)
    p.dve([], ["zrows"], "memset", z, 0.0)
    for r in range(row0, row0 + 256, 128):
        for c0 in range(0, S, 512):
            p.dma("pool", ["zrows"], ["MS"], c.MS[si][r:r + 128, c0:c0 + 512], z)


def phase_a(c, si, l):
    p, S = c.p, c.seqs[si]
    NT = S // 512
    vb = l * VL
    a1 = Arena(c.R1, 16384)
    c.hT = hT = a1.bf(8 * S).rearrange("p (k t) -> p k t", k=8)
    a3 = Arena(c.R3, 16384)
    arn = dict(sq=a3.bf(4096).rearrange("p (k t) -> p k t", k=8), rs=a3.f32(512), eps=a3.f32(1))
    p.dve([], ["eps"], "memset", arn["eps"], EPS)
    if l == 0:
        xTs = [(a3.f32(4096).rearrange("p (k t) -> p k t", k=8), "xTa0")]
        xins = [(a3.f32(4096).rearrange("p (s d) -> p s d", s=4), "xin%d" % i) for i in range(2)]
    else:
        xTs = [(a3.f32(4096).rearrange("p (k t) -> p k t", k=8), "xTa%d" % i) for i in range(2)]

    def load(t):
        if l == 0:
            xin, xik = xins[t % 2]
            p.dma("sp", [], [xik], xin, c.X[si][t * 512:(t + 1) * 512, :].rearrange("(s p) d -> p s d", p=128))
        else:
            xT, xb = xTs[t % 2]
            p.dma("sp", ["XS"], xks(xb), xT, xs_tile(c, si, t))

    load(0)
    for t in range(NT):
        if t + 1 < NT:
            load(t + 1)
        xT, xb = xTs[t % len(xTs)]
        xk = xks(xb)
        if l == 0:
            xin, xik = xins[t % 2]
            for k in range(8):
                bank, bk = c.psrot.next()
                for s in range(4):
                    p.pe([xik, "cf"], [bk], "transpose", bank[:, s * 128:(s + 1) * 128],
                         xin[:, s, k * 128:(k + 1) * 128], c.identf)
                if k % 2 == 0:
                    p.act([bk], [xk[k]], "activation", out=xT[:, k, :], in_=bank, func=AF.Copy)
                else:
                    p.dve([bk], [xk[k]], "tensor_copy", out=xT[:, k, :], in_=bank)
            p.dma("pool", xk, ["XS"], xs_tile(c, si, t), xT)
        rmsnorm(c, arn, xT, xk, vb + V_NMIX, [hT[:, k, t * 512:(t + 1) * 512] for k in range(8)],
                ["hT"] * 8)


def mixer_conv(c, si, l):
    p, S, hT, vecs = c.p, c.seqs[si], c.hT, c.vecs
    NT = S // 512
    vb = l * VL
    a3 = Arena(c.R3, 16384)
    zT = a3.bf(2 * (S + 2)).rearrange("p (f t) -> p f t", f=2)
    tmpr = Rot([(a3.f32(512), "tmpc%d" % i) for i in range(2)])
    accr = Rot([(a3.f32(512), "acc%d" % i) for i in range(2)])
    stor = Rot([(a3.bf(512), "sto%d" % i) for i in range(3)])
    p.dve([], ["zT"], "memset", zT[:, :, 0:1], 0.0)
    p.dve([], ["zT"], "memset", zT[:, :, S + 1:S + 2], 0.0)
    wA, kA = wload(c, c.WB["wfm"][l, T_CC:T_CC + 4].rearrange("c p k j -> p c (k j)"), 4096,
                   "p (g k j) -> p g k j", g=4, k=8)
    wBv, kB = wload(c, c.WB["wfm"][l, T_CB:T_CB + 2].rearrange("c p k j -> p c (k j)"), 2048,
                    "p (g k j) -> p g k j", g=2, k=8)
    for t in range(NT):
        for ft in range(2):
            bc, bck = c.psrot.next()
            bh, bhk = c.psrot.next()
            proj_fm(c, bc, bck, wA[:, ft], kA, hT, t * 512)
            proj_fm(c, bh, bhk, wA[:, 2 + ft], kA, hT, t * 512)
            tm, tmk = tmpr.next()
            p.act([bck], [tmk], "activation", out=tm, in_=bc, func=AF.Copy)
            p.dve([tmk, bhk], ["zT"], "tensor_tensor", out=zT[:, ft, 1 + t * 512:1 + (t + 1) * 512], in0=tm,
                  in1=bh, op=ALU.mult)
    for t in range(NT):
        for ft in range(2):
            bb, bbk = c.psrot.next()
            proj_fm(c, bb, bbk, wBv[:, ft], kB, hT, t * 512)
            ac, ack = accr.next()
            so, sok = stor.next()
            c0 = t * 512
            wc = lambda tap: vecs[:, vb + V_CONV + tap * 2 + ft:vb + V_CONV + tap * 2 + ft + 1]
            p.dve(["zT", "vecs"], [ack], "tensor_scalar", out=ac, in0=zT[:, ft, c0 + 1:c0 + 513], scalar1=wc(1),
                  scalar2=None, op0=ALU.mult)
            p.dve(["zT", "vecs", ack], [ack], "scalar_tensor_tensor", out=ac, in0=zT[:, ft, c0:c0 + 512],
                  scalar=wc(0), in1=ac, op0=ALU.mult, op1=ALU.add)
            p.dve(["zT", "vecs", ack], [ack], "scalar_tensor_tensor", out=ac, in0=zT[:, ft, c0 + 2:c0 + 514],
                  scalar=wc(2), in1=ac, op0=ALU.mult, op1=ALU.add)
            p.dve([ack, bbk], [sok], "tensor_tensor", out=so, in0=ac, in1=bb, op=ALU.mult)
            p.dma(c.store_q, [sok], ["MS"], c.MS[si][768 + ft * 128:768 + (ft + 1) * 128, c0:c0 + 512], so)
    p.barrier()


def phase_c(c, si, l):
    p, S, L = c.p, c.seqs[si], c.L
    NT = S // 512
    vb = l * VL
    last = (l == L - 1)
    a1 = Arena(c.R1, 16384)
    hf_raw = a1.f32(8192)
    hfT = hf_raw.bitcast(BF16).rearrange("p (f t) -> p f t", f=32)
    yo = hf_raw[:, 0:4096].rearrange("p (s d) -> p s d", s=4)
    yokeys = ["hfT%d" % f for f in range(16)]
    h2T = a1.bf(8 * 512).rearrange("p (k t) -> p k t", k=8)
    arn = dict(sq=a1.bf(4096).rearrange("p (k t) -> p k t", k=8), rs=a1.f32(512), eps=a1.f32(1))
    rlr = Rot([(a1.bf(512), "rl%d" % i) for i in range(2)])
    gts = Rot([(a1.f32(512), "gt%d" % i) for i in range(2)])
    tms = Rot([(a1.f32(512), "tmg%d" % i) for i in range(2)])
    a3 = Arena(c.R3, 16384)
    xTs = [(a3.f32(4096).rearrange("p (k t) -> p k t", k=8), "xTc%d" % i) for i in range(2)]
    mixs = [(a3.bf(4096).rearrange("p (k t) -> p k t", k=8), "mix%d" % i) for i in range(2)]
    pins = [(a3.f32(1024).rearrange("p (s d) -> p s d", s=4), "pin%d" % i) for i in range(2)]
    pT = a3.bf(1024).rearrange("p (k t) -> p k t", k=2)
    p.dve([], ["eps"], "memset", arn["eps"], EPS)
    WB = c.WB

    def c_loads(t):
        xv, xk = xTs[t % 2]
        mv, mk = mixs[t % 2]
        pv, pk = pins[t % 2]
        p.dma("sp", ["XS"], xks(xk), xv, xs_tile(c, si, t))
        p.dma("sp", ["MS"], [mk], mv, c.MS[si][:, t * 512:(t + 1) * 512].rearrange("(k p) t -> p k t", p=128))
        p.dma("sp", [], [pk], pv, c.PP[si][l, t * 512:(t + 1) * 512, :].rearrange("(s p) d -> p s d", p=128))

    pat4 = dict(use_pat="p (g k j) -> p g k j", g=4, k=8)
    items = []
    for t in range(NT):
        for g in range(2):
            items.append((t, "wout", g, WB["wout"][l, g * 4:(g + 1) * 4].rearrange("c p k j -> p c (k j)"), 4096, pat4))
        for g in range(8):
            items.append((t, "w1", g, WB["w1"][l, g * 4:(g + 1) * 4].rearrange("c p k j -> p c (k j)"), 4096, pat4))
        for m in range(8):
            items.append((t, "w2", m, WB["w2"][l, m].rearrange("p k j -> p (k j)"), 4096,
                          dict(use_pat="p (k j) -> p k j", k=32)))
        items.append((t, "wpl", 0, None, 0, None))
        for g in range(2):
            items.append((t, "wg", g, WB["wg"][l, g * 4:(g + 1) * 4].rearrange("c p k j -> p c (k j)"), 4096, pat4))
    loaded = []

    wpl_flat = a3.bf(2048)
    wplv = (wpl_flat.rearrange("p (g k j) -> p g k j", g=8, k=2), "wplb")
    p.dma("sp", ["W_wpl_%d" % l], ["wplb"], wpl_flat.rearrange("p (a b) -> p a b", a=8),
          WB["wpl"][l].rearrange("c p k j -> p c (k j)"))

    def load_next():
        if len(loaded) < len(items):
            t, nm, g, src, n, kw = items[len(loaded)]
            loaded.append(wload(c, src, n, wn=("wout", "w1", "w2", "wg", "wpl"), **kw) if src is not None else (None, None))

    rsb = (c.ps[7][:, :], "ps7")
    psrot_all = c.psrot
    c.psrot = Rot([(c.ps[i][:, :], "ps%d" % i) for i in range(7)])

    pend = []

    def norm_part(m, xv, xk, wbase, rsb):
        km = xk + "_%d" % m
        p.act([km, "vecs"], ["h2T"], "activation", out=h2T[:, m, :], in_=xv[:, m, :], func=AF.Identity,
              scale=c.vecs[:, wbase + m:wbase + m + 1])
        p.act([km], ["sq%d" % m], "activation", out=arn["sq"][:, m, :], in_=xv[:, m, :], func=AF.Square)
        pend.append(m)

    def flush():
        while pend:
            m = pend.pop(0)
            p.pe(["sq%d" % m, "cb"], [rsb[1]], "matmul", rsb[0], c.cb[:, CB_OND:CB_OND + 128], arn["sq"][:, m, :],
                 start=(m == 0), stop=(m == 7))
            if m == 7:
                p.act([rsb[1], "eps"], ["rs"], "activation", out=arn["rs"], in_=rsb[0], func=AF.Ln, bias=arn["eps"])
                p.act(["rs"], ["rs"], "activation", out=arn["rs"], in_=arn["rs"], func=AF.Exp, scale=-0.5)

    c_loads(0)
    for _ in range(3):
        load_next()
    for ii, (t, nm, g, src, n, kw) in enumerate(items):
        load_next()
        wv, wk = loaded[ii]
        xv, xk = xTs[t % 2]
        mv, mk = mixs[t % 2]
        pv, pk = pins[t % 2]
        if nm == "wout":
            if g == 0 and t + 1 < NT:
                c_loads(t + 1)
            for mi in range(4):
                m = g * 4 + mi
                bank, bk = c.psrot.next()
                for k in range(8):
                    p.pe([wk, mk], [bk], "matmul", bank, wv[:, mi, k, :], mv[:, k, :], start=(k == 0), stop=(k == 7))
                flush()
                p.dve([bk, xk + "_%d" % m], [xk + "_%d" % m], "tensor_tensor", out=xv[:, m, :], in0=bank, in1=xv[:, m, :], op=ALU.add)
                norm_part(m, xv, xk, vb + V_NFFN, rsb)
        elif nm == "w1":
            for fi in range(4):
                f = g * 4 + fi
                bank, bk = c.psrot.next()
                for k in range(8):
                    p.pe([wk, "h2T"], [bk], "matmul", bank, wv[:, fi, k, :], h2T[:, k, :], start=(k == 0),
                         stop=(k == 7))
                flush()
                rv, rk = rlr.next()
                p.dve([bk, "rs"], [rk], "scalar_tensor_tensor", out=rv, in0=bank, scalar=0.0, in1=arn["rs"],
                      op0=ALU.max, op1=ALU.mult)
                p.pool([rk], ["hfT%d" % f], "tensor_tensor", out=hfT[:, f, :], in0=rv, in1=rv, op=ALU.mult)
        elif nm == "w2":
            m = g
            bank, bk = c.psrot.next()
            for f in range(32):
                p.pe([wk, "hfT%d" % f], [bk], "matmul", bank, wv[:, f, :], hfT[:, f, :], start=(f == 0),
                     stop=(f == 31))
            flush()
            p.dve([bk, xk + "_%d" % m], [xk + "_%d" % m], "tensor_tensor", out=xv[:, m, :], in0=bank, in1=xv[:, m, :], op=ALU.add)
            norm_part(m, xv, xk, vb + V_NPL, rsb)
        elif nm == "wpl":
            for k2 in range(2):
                bank, bk = c.psrot.next()
                for s in range(4):
                    p.pe([pk, "cf"], [bk], "transpose", bank[:, s * 128:(s + 1) * 128],
                         pv[:, s, k2 * 128:(k2 + 1) * 128], c.identf)
                p.act([bk], ["pT"], "activation", out=pT[:, k2, :], in_=bank, func=AF.Copy)
            flush()
        elif nm == "wg":
            wpv, wpk = wplv
            for mi in range(4):
                m = g * 4 + mi
                bank, bk = c.psrot.next()
                for k in range(8):
                    p.pe([wk, "h2T"], [bk], "matmul", bank, wv[:, mi, k, :], h2T[:, k, :], start=(k == 0),
                         stop=(k == 7))
                gv, gk = gts.next()
                p.dve([bk, "rs"], [gk], "tensor_tensor", out=gv, in0=bank, in1=arn["rs"], op=ALU.mult)
                p.act([gk], [gk], "activation", out=gv, in_=gv, func=AF.Sigmoid)
                b2, b2k = c.psrot.next()
                for k2 in range(2):
                    p.pe([wpk, "pT"], [b2k], "matmul", b2, wpv[:, m, k2, :], pT[:, k2, :], start=(k2 == 0),
                         stop=(k2 == 1))
                tv, tk = tms.next()
                p.dve([gk, b2k], [tk], "tensor_tensor", out=tv, in0=gv, in1=b2, op=ALU.mult)
                p.pool([tk, xk + "_%d" % m], [xk + "_%d" % m], "tensor_tensor", out=xv[:, m, :], in0=xv[:, m, :], in1=tv, op=ALU.add)
            if g == 1:
                if not last:
                    p.dma("pool", xks(xk), ["XS"], xs_tile(c, si, t), xv)
                else:
                    rmsnorm(c, arn, xv, xks(xk), L * VL, [xv[:, k, :] for k in range(8)], xks(xk))
                    for s in range(4):
                        for hf in range(2):
                            bank, bk = c.psrot.next()
                            for kk in range(4):
                                p.pe(xks(xk) + ["cf"], [bk], "transpose", bank[:, kk * 128:(kk + 1) * 128],
                                     xv[:, hf * 4 + kk, s * 128:(s + 1) * 128], c.identf)
                            if hf == 0:
                                p.act([bk], yokeys, "activation", out=yo[:, s, 0:512], in_=bank, func=AF.Copy)
                            else:
                                p.dve([bk], yokeys, "tensor_copy", out=yo[:, s, 512:1024], in_=bank)
                    p.dma("pool", yokeys, ["Y"],
                          c.Y[si][t * 512:(t + 1) * 512, :].rearrange("(s p) d -> p s d", p=128), yo)
    c.psrot = psrot_all


def mixer_att(c, si, l):
    p, S, hT, vecs, cb, cf = c.p, c.seqs[si], c.hT, c.vecs, c.cb, c.cf
    NT, NCH = S // 512, S // 128
    vb = l * VL
    a3 = Arena(c.R3, 16384)
    QrT = a3.bf(2 * S).rearrange("p (i t) -> p i t", i=2)
    KmT = a3.bf(2 * S).rearrange("p (h t) -> p h t", h=2)
    Vaug = a3.bf(NCH * 2 * 66).rearrange("p (n h d) -> p n h d", n=NCH, h=2)
    eps = a3.f32(1)
    off0 = a3.off
    trigs = [(a3.f32(1024).rearrange("p (a t) -> p a t", a=2), "atrig%d" % i) for i in range(2)]
    sqr = Rot([(a3.bf(512), "asq%d" % i) for i in range(2)])
    qnr = Rot([(a3.bf(512), "aqn%d" % i) for i in range(2)])
    rsr = Rot([(a3.f32(512), "ars%d" % i) for i in range(1)])
    t1r = Rot([(a3.f32(512), "at1%d" % i) for i in range(2)])
    t2r = Rot([(a3.f32(512), "at2%d" % i) for i in range(2)])
    a3.off = off0
    ptr = Rot([(a3.bf(512), "apt%d" % i) for i in range(3)])
    rrr = Rot([(a3.f32(512), "arr%d" % i) for i in range(2)])
    osr = Rot([(a3.f32(512), "aos%d" % i) for i in range(2)])
    ostr = Rot([(a3.bf(512), "aost%d" % i) for i in range(2)])
    p.dve([], ["QK"], "memset", KmT[:, 0, :], 0.0)
    p.pool([], ["QK"], "memset", KmT[:, 1, :], 0.0)
    p.dve([], ["eps"], "memset", eps, EPS)
    p.dve([], ["Vaug"], "memset", Vaug[:, :, :, 64:65], 1.0)
    wqk, kqk = wload(c, c.WB["wfm"][l, T_AQ:T_AQ + 3].rearrange("c p k j -> p c (k j)"), 3072,
                     "p (g k j) -> p g k j", g=3, k=8)
    wv_, kv_ = wload(c, c.WB["wav"][l].rearrange("p k j -> p (k j)"), 1024, "p (k j) -> p k j", wn=("wav",), k=8)
    for kc in range(NCH):
        bank, bk = c.psrot.next()
        for k in range(8):
            p.pe([kv_], [bk], "matmul", bank[:, 0:128], hT[:, k, kc * 128:(kc + 1) * 128], wv_[:, k, :],
                 start=(k == 0), stop=(k == 7))
        src = bank[:, 0:128].rearrange("p (h d) -> p h d", h=2)
        if kc % 2 == 0:
            p.act([bk], ["Vaug"], "activation", out=Vaug[:, kc, :, 0:64], in_=src, func=AF.Copy)
        else:
            p.dve([bk], ["Vaug"], "tensor_copy", out=Vaug[:, kc, :, 0:64], in_=src)
    blocks = [(t, j) for t in range(NT) for j in range(3)]
    pj = {}

    def emit_proj(bi):
        t, j = blocks[bi]
        bank, bk = c.psrot.next()
        proj_fm(c, bank, bk, wqk[:, j], kqk, hT, t * 512)
        pj[bi] = (bank, bk)

    def load_trig(t):
        p.dma("sp", [], [trigs[t % 2][1]], trigs[t % 2][0],
              c.ROPEA[:, :, t * 512:(t + 1) * 512].rearrange("a p t -> p a t"))

    load_trig(0)
    for bi in range(min(2, len(blocks))):
        emit_proj(bi)
    for bi, (t, j) in enumerate(blocks):
        c0 = t * 512
        if j == 0 and t + 1 < NT:
            load_trig(t + 1)
        trig, trk = trigs[t % 2]
        bank, bk = pj.pop(bi)
        sq, sqk = sqr.next()
        p.act([bk], [sqk], "activation", out=sq, in_=bank, func=AF.Square)
        b2, b2k = c.psrot.next()
        p.pe([sqk, "cb"], [b2k], "matmul", b2, cb[:, CB_BO:CB_BO + 128], sq, start=True, stop=True)
        rs, rsk = rsr.next()
        p.act([b2k, "eps"], [rsk], "activation", out=rs, in_=b2, func=AF.Ln, bias=eps)
        p.act([rsk], [rsk], "activation", out=rs, in_=rs, func=AF.Exp, scale=-0.5)
        qn, qnk = qnr.next()
        wcol = vb + (V_QNW if j < 2 else V_KNW)
        p.dve([bk, rsk, "vecs"], [qnk], "scalar_tensor_tensor", out=qn, in0=bank,
              scalar=vecs[:, wcol:wcol + 1], in1=rs, op0=ALU.mult, op1=ALU.mult)
        if bi + 2 < len(blocks):
            emit_proj(bi + 2)
        b3, b3k = c.psrot.next()
        p.pe([qnk, "cb"], [b3k], "matmul", b3, cb[:, CB_P16:CB_P16 + 128], qn, start=True, stop=True)
        t1, t1k = t1r.next()
        t2, t2k = t2r.next()
        p.pool([qnk, trk], [t1k], "tensor_tensor", out=t1, in0=qn, in1=trig[:, 0, :], op=ALU.mult)
        p.dve([b3k, trk], [t2k], "tensor_tensor", out=t2, in0=b3, in1=trig[:, 1, :], op=ALU.mult)
        if j < 2:
            p.dve([t1k, t2k], ["QK"], "tensor_tensor", out=QrT[:, j, c0:c0 + 512], in0=t1, in1=t2, op=ALU.add)
        else:
            for hv in range(2):
                h0 = hv * 64
                p.pool([t1k, t2k], ["QK"], "tensor_tensor", out=KmT[h0:h0 + 64, hv, c0:c0 + 512],
                       in0=t1[h0:h0 + 64, :], in1=t2[h0:h0 + 64, :], op=ALU.add)
    p.barrier()
    if getattr(c, "bg_cast", None) is not None:
        c.bg_cast()
        c.bg_cast = None
    psS = Rot([(c.ps[i][:, :], "ps%d" % i) for i in range(4)])
    psO = Rot([(c.ps[i][:, :], "ps%d" % i) for i in (4, 5, 6, 7)])
    steps = [(qt, i, hv, kc) for qt in range(NT) for i in range(2) for hv in range(2) for kc in range(NCH)]
    deferred = []
    LA = 2

    def emit_scores(j):
        qt, i, hv, kc = steps[j]
        q0 = qt * 512
        h0 = hv * 64
        bs, bsk = psS.next()
        p.pe(["QK"], [bsk], "matmul", bs, KmT[:, hv, kc * 128:(kc + 1) * 128], QrT[:, i, q0:q0 + 512],
             start=True, stop=True)
        return (bs, bsk)

    def fin1(qt, i, hv, bo, bok):
        osb, osk = osr.next()
        p.dve([bok], [osk], "tensor_copy", out=osb[0:65, :], in_=bo[0:65, :])
        rr, rrk = rrr.next()
        p.dve([osk], [rrk], "reciprocal", out=rr[64:65, :], in_=osb[64:65, :])
        return (qt, i, hv, osb, osk, rr, rrk)

    def fin2(qt, i, hv, osb, osk, rr, rrk):
        br, brk = psS.next()
        p.pe([rrk, "cf"], [brk], "matmul", br[0:64, :], cf[64:65, CF_ONES:CF_ONES + 64], rr[64:65, :],
             start=True, stop=True)
        ost, ostk = ostr.next()
        p.dve([osk, brk], [ostk], "tensor_tensor", out=ost[0:64, :], in0=osb[0:64, :], in1=br[0:64, :],
              op=ALU.mult)
        r0 = 512 + i * 128 + hv * 64
        p.dma(c.store_q, [ostk], ["MS"], c.MS[si][r0:r0 + 64, qt * 512:(qt + 1) * 512], ost[0:64, :])

    scq = [emit_scores(j) for j in range(min(LA, len(steps)))]
    cur = None
    for j, (qt, i, hv, kc) in enumerate(steps):
        if j + LA < len(steps):
            scq.append(emit_scores(j + LA))
        bs, bsk = scq[j]
        if kc == 0:
            cur = psO.next()
        pt, ptk = ptr.next()
        p.act([bsk], [ptk], "activation", out=pt, in_=bs, func=AF.Exp, scale=0.125)
        bo, bok = cur
        p.pe([ptk, "Vaug"], [bok], "matmul", bo[0:65, :], Vaug[:, kc, hv, 0:65], pt,
             start=(kc == 0), stop=(kc == NCH - 1))
        for dj, args in list(deferred):
            if dj <= j:
                fin2(*args)
                deferred.remove((dj, args))
        if kc == NCH - 1:
            deferred.append((j + 6, fin1(qt, i, hv, bo, bok)))
    for dj, args in deferred:
        fin2(*args)
    p.barrier()


def mixer_ret(c, si, l):
    p, S, hT, vecs, cb, cf, lg = c.p, c.seqs[si], c.hT, c.vecs, c.cb, c.cf, c.lg
    NT, NCH = S // 512, S // 128
    vb = l * VL
    for pr in range(2):
        a3 = Arena(c.R3, 16384)
        QrT = a3.bf(S)
        KrT = a3.bf(S)
        Vt = a3.bf(NCH * 128).rearrange("p (n d) -> p n d", n=NCH)
        SF = a3.bf(NCH * 64).rearrange("p (n d) -> p n d", n=NCH)
        SB = a3.bf(NCH * 64).rearrange("p (n d) -> p n d", n=NCH)
        trig = a3.f32(1024).rearrange("p (a t) -> p a t", a=2)
        f5 = Rot([(a3.f32(512), "rf5%d" % i) for i in range(4)])
        qbr = Rot([(a3.bf(512), "rqb%d" % i) for i in range(2)])
        DT = a3.f32(256)
        ZF, ZB, XF, XB = a3.f32(128), a3.f32(128), a3.f32(128), a3.f32(128)
        gF, gB, eps = a3.f32(1), a3.f32(1), a3.f32(1)
        kzr = Rot([(a3.bf(128), "rkz%d" % i) for i in range(4)])
        stF, stB, stF2, stB2 = a3.f32(128), a3.f32(128), a3.f32(128), a3.f32(128)
        XBb = a3.bf(128)
        off_ov = a3.off
        ptr = Rot([(a3.bf(256), "rpt%d" % i) for i in range(4)])
        qxr = Rot([(a3.bf(512).rearrange("p (c i) -> p c i", c=4), "rqx%d" % i) for i in range(2)])
        assert a3.off - off_ov == 1024
        trigs = [(trig, "rtrig0"), (c.R3[:, off_ov:off_ov + 1024].rearrange("p (a t) -> p a t", a=2), "rtrig1")]
        obr = Rot([(a3.bf(512), "rob%d" % i) for i in range(2)])
        dsr = Rot([(a3.bf(512), "rds%d" % i) for i in range(1)])
        sgr = Rot([(a3.bf(512), "rsg%d" % i) for i in range(1)])
        ysr = Rot([(a3.bf(512), "rys%d" % i) for i in range(2)])
        p.dve([], ["eps"], "memset", eps, EPS)
        cF = 16 + (l * 2 + 0) * 2 + pr
        cB = 16 + (l * 2 + 1) * 2 + pr
        lgFc, lgBc = lg[:, cF:cF + 1], lg[:, cB:cB + 1]
        rF = 24 + ((l * 2 + 0) * 2 + pr) * 128
        rB = 24 + ((l * 2 + 1) * 2 + pr) * 128
        tb, tbk = f5.next()
        for hh in range(2):
            hF = (l * 2 + 0) * 4 + 2 * pr + hh
            hB = (l * 2 + 1) * 4 + 2 * pr + hh
            tv = tb[:, hh * 128:(hh + 1) * 128]
            p.dve(["cf", "lg"], [tbk], "tensor_scalar", out=tv, in0=cf[:, CF_T1:CF_T1 + 128],
                  scalar1=lg[:, hF:hF + 1], scalar2=None, op0=ALU.mult)
            p.dve(["cf", "lg", tbk], [tbk], "scalar_tensor_tensor", out=tv, in0=cf[:, CF_T2:CF_T2 + 128],
                  scalar=lg[:, hB:hB + 1], in1=tv, op0=ALU.mult, op1=ALU.add)
        p.act([tbk], ["rtab"], "activation", out=DT, in_=tb[:, 0:256], func=AF.Exp)
        p.act(["lg", "cf"], ["rtab"], "activation", out=ZF, in_=lg[:, rF:rF + 128], func=AF.Exp,
              scale=cf[:, CF_C127:CF_C127 + 1])
        p.act(["lg", "cf"], ["rtab"], "activation", out=ZB, in_=lg[:, rB:rB + 128], func=AF.Exp,
              scale=cf[:, CF_CJ:CF_CJ + 1])
        p.act(["lg", "cf"], ["rtab"], "activation", out=XF, in_=cf[:, CF_IP1:CF_IP1 + 128], func=AF.Exp, scale=lgFc)
        p.act(["lg", "cf"], ["rtab"], "activation", out=XB, in_=cf[:, CF_IM:CF_IM + 128], func=AF.Exp, scale=lgBc)
        p.act(["lg", "cf"], ["rtabb"], "activation", out=XBb, in_=cf[:, CF_IM:CF_IM + 128], func=AF.Exp, scale=lgBc)
        p.act(["lg"], ["rtab"], "activation", out=gF, in_=lgFc, func=AF.Exp, scale=128.0)
        p.act(["lg"], ["rtab"], "activation", out=gB, in_=lgBc, func=AF.Exp, scale=128.0)
        wq, kq = wload(c, c.WB["wfm"][l, pr:6:2].rearrange("c p k j -> p c (k j)"), 3072,
                       "p (g k j) -> p g k j", g=3, k=8)
        wv_, kv_ = wload(c, c.WB["wrv"][l][:, :, pr * 128:(pr + 1) * 128], 1024, "p (k j) -> p k j", wn=("wrv",), k=8)
        blocks = [(t, j) for t in range(NT) for j in range(2)]
        pj = {}

        def emit_proj(bi):
            t, j = blocks[bi]
            bank, bk = c.psrot.next()
            proj_fm(c, bank, bk, wq[:, j], kq, hT, t * 512)
            pj[bi] = (bank, bk)

        def load_trig(t):
            p.dma("sp", [], [trigs[t % 2][1]], trigs[t % 2][0],
                  c.ROPER[:, :, t * 512:(t + 1) * 512].rearrange("a p t -> p a t"))

        load_trig(0)
        emit_proj(0)
        for bi, (t, j) in enumerate(blocks):
            c0 = t * 512
            if j == 0 and t + 1 < NT:
                load_trig(t + 1)
            trg, trk = trigs[t % 2]
            bank, bk = pj.pop(bi)
            qb, qbk = qbr.next()
            p.act([bk], [qbk], "activation", out=qb, in_=bank, func=AF.Identity,
                  scale=(1.0 if j == 0 else 0.125))
            if bi + 1 < len(blocks):
                emit_proj(bi + 1)
            b3, b3k = c.psrot.next()
            p.pe([qbk, "cb"], [b3k], "matmul", b3, cb[:, CB_P32:CB_P32 + 128], qb, start=True, stop=True)
            t1, t1k = f5.next()
            t2, t2k = f5.next()
            p.pool([qbk, trk], [t1k], "tensor_tensor", out=t1, in0=qb, in1=trg[:, 0, :], op=ALU.mult)
            p.dve([b3k, trk], [t2k], "tensor_tensor", out=t2, in0=b3, in1=trg[:, 1, :], op=ALU.mult)
            dst = (QrT if j == 0 else KrT)[:, c0:c0 + 512]
            p.dve([t1k, t2k], ["QK"], "tensor_tensor", out=dst, in0=t1, in1=t2, op=ALU.add)
        p.barrier()
        for n in range(NCH):
            bank, bk = c.psrot.next()
            for k in range(8):
                p.pe([kv_], [bk], "matmul", bank[:, 0:128], hT[:, k, n * 128:(n + 1) * 128], wv_[:, k, :],
                     start=(k == 0), stop=(k == 7))
            if n % 2 == 0:
                p.act([bk], ["Vt"], "activation", out=Vt[:, n, :], in_=bank[:, 0:128], func=AF.Copy)
            else:
                p.dve([bk], ["Vt"], "tensor_copy", out=Vt[:, n, :], in_=bank[:, 0:128])
        dirs = ((ZF, gF, SF), (ZB, gB, SB))
        sts = ((stF, stF2), (stB, stB2))
        for dr in range(2):
            p.dve([], ["rst%d_0" % dr], "memset", sts[dr][0], 0.0)
        items = []
        for q in range(NCH):
            items.append((0, q, q))
            items.append((1, NCH - 1 - q, q))
        bTs, kzs = {}, {}

        def emit_T(ii):
            dr, n, q = items[ii]
            bT, bTk = c.psrot.next()
            p.pe(["QK", "cb"], [bTk], "matmul", bT[:, 0:128], KrT[:, n * 128:(n + 1) * 128], c.identb,
                 start=True, stop=True)
            bTs[ii] = (bT, bTk)

        def emit_kz(ii):
            dr, n, q = items[ii]
            bT, bTk = bTs.pop(ii)
            kz, kzk = kzr.next()
            p.dve([bTk, "rtab"], [kzk], "tensor_tensor", out=kz, in0=bT[:, 0:128], in1=dirs[dr][0], op=ALU.mult)
            kzs[ii] = (kz, kzk)

        emit_T(0)
        emit_T(1)
        emit_kz(0)
        for ii, (dr, n, q) in enumerate(items):
            if ii + 2 < len(items):
                emit_T(ii + 2)
            if ii + 1 < len(items):
                emit_kz(ii + 1)
            Z, gcol, Sx = dirs[dr]
            cur, nx = sts[dr][q % 2], sts[dr][(q + 1) % 2]
            ck, nk = "rst%d_%d" % (dr, q % 2), "rst%d_%d" % (dr, (q + 1) % 2)
            kz, kzk = kzs.pop(ii)
            bkv, bkvk = c.psrot.next()
            p.pe([kzk, "Vt"], [bkvk], "matmul", bkv[:, 0:128], kz, Vt[:, n, :], start=True, stop=True)
            p.act([ck], ["rS%d" % dr], "activation", out=Sx[0:64, n, :], in_=cur[0:64, 0:64], func=AF.Copy)
            p.pool([ck], ["rS%d" % dr], "tensor_copy", out=Sx[64:128, n, :], in_=cur[64:128, 64:128])
            p.dve([ck, bkvk, "rtab"], [nk], "scalar_tensor_tensor", out=nx, in0=cur, scalar=gcol,
                  in1=bkv[:, 0:128], op0=ALU.mult, op1=ALU.add)
        def chunk_work(t):
            c0 = t * 512
            qxF, qxFk = qxr.next()
            qxB, qxBk = qxr.next()
            qv = QrT[:, c0:c0 + 512].rearrange("p (c i) -> p c i", c=4)
            for cn in range(4):
                p.dve(["QK", "rtab"], [qxFk], "tensor_tensor", out=qxF[:, cn, :], in0=qv[:, cn, :], in1=XF,
                      op=ALU.mult)
                p.pool(["QK", "rtabb"], [qxBk], "tensor_tensor", out=qxB[:, cn, :], in0=qv[:, cn, :], in1=XBb,
                       op=ALU.mult)
            bOs = [c.psrot.next(), c.psrot.next()]
            bXs = [c.psrot.next(), c.psrot.next()]
            bSs = [c.psrot.next(), c.psrot.next()]
            pts = []
            for cn in range(4):
                n = t * 4 + cn
                cs = slice(n * 128, (n + 1) * 128)
                for hh in range(2):
                    h0 = hh * 64
                    bS, bSk = bSs[hh]
                    p.pe(["QK"], [bSk], "matmul", bS[:, cn * 128:(cn + 1) * 128], KrT[h0:h0 + 64, cs],
                         QrT[h0:h0 + 64, cs], start=True, stop=True)
            for cn in range(4):
                pt, ptk = ptr.next()
                pts.append((pt, ptk))
                for hh in range(2):
                    bS, bSk = bSs[hh]
                    p.dve([bSk, "rtab"], [ptk], "tensor_tensor", out=pt[:, hh * 128:(hh + 1) * 128],
                          in0=bS[:, cn * 128:(cn + 1) * 128], in1=DT[:, hh * 128:(hh + 1) * 128], op=ALU.mult)
            for cn in range(4):
                n = t * 4 + cn
                pt, ptk = pts[cn]
                for hh in range(2):
                    h0 = hh * 64
                    bX, bXk = bXs[hh]
                    xc = bX[h0:h0 + 64, cn * 128:(cn + 1) * 128]
                    bOc, bOck = bOs[cn // 2]
                    p.pe([ptk, "Vt"], [bOck], "matmul", bOc[:, ((cn % 2) * 2 + hh) * 128:((cn % 2) * 2 + hh + 1) * 128],
                         Vt[:, n, :], pt[:, hh * 128:(hh + 1) * 128], start=True, stop=True)
                    p.pe([qxFk, "rS0"], [bXk], "matmul", xc, SF[h0:h0 + 64, n, :], qxF[h0:h0 + 64, cn, :],
                         start=True, stop=False)
                    p.pe([qxBk, "rS1"], [bXk], "matmul", xc, SB[h0:h0 + 64, n, :], qxB[h0:h0 + 64, cn, :],
                         start=False, stop=True)
            oi, oik = f5.next()
            for b2 in range(2):
                bOc, bOck = bOs[b2]
                for hh in range(2):
                    h0 = hh * 64
                    p.act([bOck], [oik], "activation",
                          out=oi[h0:h0 + 64, b2 * 256:(b2 + 1) * 256].rearrange("p (c i) -> p c i", c=2),
                          in_=bOc[h0:h0 + 64, :].rearrange("p (c h i) -> p c h i", c=2, h=2)[:, :, hh, :],
                          func=AF.Copy)
            ob, obk = obr.next()
            for hh in range(2):
                h0 = hh * 64
                p.dve([oik, bXs[hh][1]], [obk], "tensor_tensor", out=ob[h0:h0 + 64, :], in0=oi[h0:h0 + 64, :],
                      in1=bXs[hh][0][h0:h0 + 64, :], op=ALU.add)
            return ob, obk

        def out_chain(t, ob, obk):
            c0 = t * 512
            bM, bMk = c.psrot.next()
            p.pe([obk, "cb"], [bMk], "matmul", bM, cb[:, CB_BO:CB_BO + 128], ob, start=True, stop=True)
            d, dk = f5.next()
            p.dve([obk, bMk], [dk], "tensor_tensor", out=d, in0=ob, in1=bM, op=ALU.subtract)
            ds, dsk = dsr.next()
            p.act([dk], [dsk], "activation", out=ds, in_=d, func=AF.Square)
            bG, bGk = c.psrot.next()
            proj_fm(c, bG, bGk, wq[:, 2], kq, hT, c0)
            sg, sgk = sgr.next()
            p.act([bGk], [sgk], "activation", out=sg, in_=bG, func=AF.Silu)
            bV, bVk = c.psrot.next()
            p.pe([dsk, "cb"], [bVk], "matmul", bV, cb[:, CB_BO:CB_BO + 128], ds, start=True, stop=True)
            rs, rsk = f5.next()
            p.act([bVk, "eps"], [rsk], "activation", out=rs, in_=bV, func=AF.Ln, bias=eps)
            p.act([rsk], [rsk], "activation", out=rs, in_=rs, func=AF.Exp, scale=-0.5)
            p.pool([dk, rsk], [dk], "tensor_tensor", out=d, in0=d, in1=rs, op=ALU.mult)
            ys, ysk = ysr.next()
            p.dve([dk, sgk, "vecs"], [ysk], "scalar_tensor_tensor", out=ys, in0=d,
                  scalar=vecs[:, vb + V_GNW + pr:vb + V_GNW + pr + 1], in1=sg, op0=ALU.mult, op1=ALU.mult)
            p.dma(c.store_q, [ysk], ["MS"], c.MS[si][pr * 128:(pr + 1) * 128, c0:c0 + 512], ys)

        prev = None
        for t in range(NT):
            cw = chunk_work(t)
            if prev is not None:
                out_chain(t - 1, *prev)
            prev = cw
        out_chain(NT - 1, *prev)
        p.barrier()


def mixer_fft(c, si, l):
    p, S, hT, cb, cf = c.p, c.seqs[si], c.hT, c.cb, c.cf
    nb = S // 128
    fs = 128 // nb
    ng = nb
    sti = c.stypes.index(S)
    Tc = cf[:, CF_TW + sti * 256:CF_TW + sti * 256 + 128]
    Ts = cf[:, CF_TW + sti * 256 + 128:CF_TW + sti * 256 + 256]
    KBc = cb[:, CB_KB + sti * 256:CB_KB + sti * 256 + 128]
    KBs = cb[:, CB_KB + sti * 256 + 128:CB_KB + sti * 256 + 256]
    SA1 = cb[:, CB_SA1:CB_SA1 + 256]
    SA2 = cb[:, CB_SA2:CB_SA2 + 256]
    a3 = Arena(c.R3, 16384)
    ABts = [a3.bf(nb * 256).rearrange("p (r g b f) -> p r g b f", r=2, g=ng, b=nb) for _ in range(2)]
    x2r = Rot([(a3.bf(512).rearrange("p (g r c) -> p g r c", g=2, r=2), "fx%d" % i) for i in range(4)])
    Yts = [a3.bf(nb * 128).rearrange("p (d f) -> p d f", d=nb) for _ in range(2)]
    m1r = Rot([(a3.f32(512), "fm1%d" % i) for i in range(2)])
    m2r = Rot([(a3.f32(512), "fm2%d" % i) for i in range(2)])
    stg = Rot([(a3.bf(512), "fst%d" % i) for i in range(2)])
    wabs = [wload(c, c.WAB[l, jh].rearrange("p k j -> p (k j)"), 2048, "p (k j) -> p k j",
                  wn=("W_wab_%d_%d" % (l, jh),), k=8) for jh in range(2)]
    st = {"s1bank": None, "bbank": {}}

    def step1(jh, b):
        wab, kab = wabs[jh]
        ABt = ABts[jh]
        if b % 2 == 0:
            st["s1bank"] = c.psrot.next()
        bank, bk = st["s1bank"]
        reg = bank[:, (b % 2) * 256:(b % 2 + 1) * 256]
        for k in range(8):
            p.pe([kab], [bk], "matmul", reg, hT[:, k, b:S:nb], wab[:, k, :], start=(k == 0), stop=(k == 7))
        for r in range(2):
            src = reg[:, r * 128:(r + 1) * 128].rearrange("p (g f) -> p g f", g=ng)
            if r == 0:
                p.act([bk], ["ABt%d" % jh], "activation", out=ABt[:, r, :, b, :], in_=src, func=AF.Copy)
            else:
                p.dve([bk], ["ABt%d" % jh], "tensor_copy", out=ABt[:, r, :, b, :], in_=src)

    def stage_a(jh, i):
        ABt = ABts[jh]
        bank, bk = c.psrot.next()
        for gl in range(2):
            g = 2 * i + gl
            reg = bank[:, gl * 256:(gl + 1) * 256]
            p.pe(["ABt%d" % jh, "cb"], [bk], "matmul", reg, ABt[:, 0, g].rearrange("p b f -> p (b f)"), SA1,
                 start=True, stop=False)
            p.pe(["ABt%d" % jh, "cb"], [bk], "matmul", reg, ABt[:, 1, g].rearrange("p b f -> p (b f)"), SA2,
                 start=False, stop=True)
        m1, m1k = m1r.next()
        m2, m2k = m2r.next()
        bv = bank.rearrange("p (q c) -> p q c", q=4)
        p.dve([bk, "cf"], [m1k], "tensor_tensor", out=m1.rearrange("p (q c) -> p q c", q=4), in0=bv,
              in1=Tc.unsqueeze(1).broadcast_to([128, 4, 128]), op=ALU.mult)
        p.dve([bk, "cf"], [m2k], "tensor_tensor", out=m2.rearrange("p (q c) -> p q c", q=4), in0=bv,
              in1=Ts.unsqueeze(1).broadcast_to([128, 4, 128]), op=ALU.mult)
        m1v = m1.rearrange("p (g r c) -> p g r c", g=2, r=2)
        m2v = m2.rearrange("p (g r c) -> p g r c", g=2, r=2)
        x2, x2k = x2r.next()
        p.pool([m1k, m2k], [x2k], "tensor_tensor", out=x2[:, :, 0, :], in0=m1v[:, :, 0, :], in1=m2v[:, :, 1, :],
               op=ALU.subtract)
        p.pool([m1k, m2k], [x2k], "tensor_tensor", out=x2[:, :, 1, :], in0=m2v[:, :, 0, :], in1=m1v[:, :, 1, :],
               op=ALU.add)
        return x2, x2k

    def stage_b(jh, i, x2, x2k):
        Yt = Yts[jh]
        if i % 2 == 0:
            st["bbank"][jh] = c.psrot.next()
        bank, bk = st["bbank"][jh]
        for gl in range(2):
            g = 2 * i + gl
            reg = bank[:, (g % 4) * 128:(g % 4 + 1) * 128]
            p.pe([x2k, "cb"], [bk], "matmul", reg, x2[:, gl, 0, :], KBc, start=True, stop=False)
            p.pe([x2k, "cb"], [bk], "matmul", reg, x2[:, gl, 1, :], KBs, start=False, stop=True)
            src = reg.rearrange("p (d f) -> p d f", d=nb)
            if g % 2 == 0:
                p.act([bk], ["Yt%d" % jh], "activation", out=Yt[:, :, g * fs:(g + 1) * fs], in_=src, func=AF.Copy)
            else:
                p.dve([bk], ["Yt%d" % jh], "tensor_copy", out=Yt[:, :, g * fs:(g + 1) * fs], in_=src)

    def transposes(jh, d0):
        Yt = Yts[jh]
        bank, bk = c.psrot.next()
        for dl in range(4):
            p.pe(["Yt%d" % jh, "cb"], [bk], "matmul", bank[:, dl * 128:(dl + 1) * 128], Yt[:, d0 + dl, :], c.identb,
                 start=True, stop=True)
        so, sok = stg.next()
        if (d0 // 4) % 2 == 0:
            p.act([bk], [sok], "activation", out=so, in_=bank, func=AF.Copy)
        else:
            p.dve([bk], [sok], "tensor_copy", out=so, in_=bank)
        p.dma(c.store_q, [sok], ["MS"], c.MS[si][256 + jh * 128:256 + (jh + 1) * 128, d0 * 128:(d0 + 4) * 128], so)

    for b in range(nb):
        step1(0, b)
    pend = None
    for i in range(ng // 2):
        step1(1, 2 * i)
        step1(1, 2 * i + 1)
        x2 = stage_a(0, i)
        if pend is not None:
            stage_b(0, *pend)
        pend = (i,) + x2
    stage_b(0, *pend)
    pend = None
    tdone = 0
    for i in range(ng // 2):
        x2 = stage_a(1, i)
        if pend is not None:
            stage_b(1, *pend)
        pend = (i,) + x2
        if i % 2 == 0 and tdone < nb:
            transposes(0, tdone)
            tdone += 4
    stage_b(1, *pend)
    while tdone < nb:
        transposes(0, tdone)
        tdone += 4
    for d0 in range(0, nb, 4):
        transposes(1, d0)
    p.barrier()


_CACHE = {}


def run(inputs, seqs_per_core, ncores, mixers=("ret", "fft", "att", "conv"), core_seq_arrays=None):
    L = int(np.asarray(inputs["w_in"]).shape[0])
    key = (tuple(seqs_per_core), L, tuple(mixers))
    if key not in _CACHE:
        _CACHE[key] = build(list(seqs_per_core), L, mixers)
    nc, stats = _CACHE[key]
    stypes = sorted(set(seqs_per_core))
    cf, cb, rope_r, rope_a = _consts(stypes, max(seqs_per_core))
    w = _prep_weights(inputs, L)
    shared = {"wfut": w["wfut"], "vecs": w["vecs"], "rates": w["rates"], "cf": cf, "cb": cb,
              "rope_r": rope_r, "rope_a": rope_a}
    for n in WSPECS:
        shared[n + "_f"] = w[n]
    in_maps = []
    for c in range(ncores):
        m = dict(shared)
        for i, (xa, pa) in enumerate(core_seq_arrays[c]):
            m["x%d" % i] = np.ascontiguousarray(xa, dtype=np.float32)
            m["p%d" % i] = np.ascontiguousarray(pa, dtype=np.float32)
        in_maps.append(m)
    res = run_bass_kernel_spmd(nc, in_maps, core_ids=list(range(ncores)))
    return [[np.asarray(res.results[c]["y%d" % i]) for i in range(len(seqs_per_core))] for c in range(ncores)], stats


def kernel(x_prompt, x_sample, p_prompt, p_sample, norm_mix_w, w_in, ret_log_rate, ret_gn_w, q_norm_w, k_norm_w,
           conv_w, w_out, norm_ffn_w, w_ffn1, w_ffn2, norm_pl_w, w_pl_gate, w_pl_proj, final_norm_w):
    inputs = dict(norm_mix_w=norm_mix_w, w_in=w_in, ret_log_rate=ret_log_rate, ret_gn_w=ret_gn_w,
                  q_norm_w=q_norm_w, k_norm_w=k_norm_w, conv_w=conv_w, w_out=w_out, norm_ffn_w=norm_ffn_w,
                  w_ffn1=w_ffn1, w_ffn2=w_ffn2, norm_pl_w=norm_pl_w, w_pl_gate=w_pl_gate, w_pl_proj=w_pl_proj,
                  final_norm_w=final_norm_w)
    inputs = {k: np.asarray(v) for k, v in inputs.items()}
    x_prompt = np.asarray(x_prompt)
    x_sample = np.asarray(x_sample)
    p_prompt = np.asarray(p_prompt)
    p_sample = np.asarray(p_sample)
    B, SP, _ = x_prompt.shape
    BS, SS, _ = x_sample.shape
    npc = B // NCORES
    nsc = BS // NCORES
    seqs = [SP] * npc + [SS] * nsc
    arrs = []
    for c in range(NCORES):
        a = []
        for j in range(npc):
            b = c * npc + j
            a.append((x_prompt[b], p_prompt[:, b]))
        for j in range(nsc):
            b = c * nsc + j
            a.append((x_sample[b], p_sample[:, b]))
        arrs.append(a)
    outs, _ = run(inputs, seqs, NCORES, core_seq_arrays=arrs)
    y_prompt = np.empty((B, SP, D), np.float32)
    y_sample = np.empty((BS, SS, D), np.float32)
    for c in range(NCORES):
        for j in range(npc):
            y_prompt[c * npc + j] = outs[c][j]
        for j in range(nsc):
            y_sample[c * nsc + j] = outs[c][npc + j]
    return (y_prompt, y_sample)
```
